# Optimizing a Trainium2 kernel written in Bass

```python
import jax, jax.numpy as jnp
from jax import lax
import numpy as np

D_MODEL = 1024
BATCH = 16
SEQ = 2048
DEPTH = 2
DEC_BATCH = 32
DEC_SEQ = 32
PAST_LEN = 1024

CHUNK = 64
HEAD_DIM = 64
H_A = 8
KV_A = 2
WINDOW_A = 128
N_PREV_A = WINDOW_A // CHUNK
H_B = 8
N_PREV_B = 8
BAND_B = N_PREV_B * CHUNK
MAX_REL = 128
ROPE_THETA = 10000.0
D_LRU = D_MODEL
N_BLOCKS = 16
BLOCK = D_LRU // N_BLOCKS
CONV_W = 4
C_GATE = 8.0
N_AB = (DEPTH + 1) // 2
N_LRU = DEPTH // 2
EPS = 1e-6
PROJ_SIZES = (H_A * HEAD_DIM, KV_A * HEAD_DIM, KV_A * HEAD_DIM, H_A * HEAD_DIM,
              H_B * HEAD_DIM, H_B * HEAD_DIM, H_B * HEAD_DIM, H_B * HEAD_DIM)
IN_AB = sum(PROJ_SIZES)
MIX_AB = (H_A + H_B) * HEAD_DIM

kernel_name = 'hybrid_stream_swa_band_rglru_step'


def _split_cols(y, sizes):
    offs, acc = [], 0
    for s in sizes[:-1]:
        acc += s
        offs.append(acc)
    return jnp.split(y, offs, axis=-1)


def rms_norm(x, g):
    xf = x.astype(jnp.float32)
    y = xf * lax.rsqrt(jnp.mean(xf * xf, axis=-1, keepdims=True) + EPS)
    return (y * g.astype(jnp.float32)).astype(x.dtype)


def rope(x, pos):
    half = x.shape[-1] // 2
    inv = ROPE_THETA ** (-jnp.arange(half, dtype=jnp.float32) / half)
    ang = pos.astype(jnp.float32)[:, None] * inv[None, :]
    cos = jnp.cos(ang)[None, :, None, :]
    sin = jnp.sin(ang)[None, :, None, :]
    x1 = x[..., :half].astype(jnp.float32)
    x2 = x[..., half:].astype(jnp.float32)
    return jnp.concatenate([x1 * cos - x2 * sin, x2 * cos + x1 * sin], axis=-1).astype(x.dtype)


def attend(q, k, v, q_pos, k_pos, valid, bias_table, sinks):
    _, tq, kvh, g, dh = q.shape
    tk = k.shape[1]
    s = jnp.einsum('bqhgd,bshd->bhgqs', q, k).astype(jnp.float32) * (dh ** -0.5)
    if bias_table is not None:
        rel = jnp.clip(q_pos[:, None] - k_pos[None, :], -MAX_REL, MAX_REL) + MAX_REL
        bias = bias_table.astype(jnp.float32)[:, rel]
        s = s + bias.reshape(kvh, g, tq, tk)
    if valid is not None:
        s = jnp.where(valid, s, -1e30)
    if sinks is not None:
        sink = sinks.astype(jnp.float32).reshape(kvh, g)[None, :, :, None, None]
        m = jnp.maximum(jnp.max(s, axis=-1, keepdims=True), sink)
        p = jnp.exp(s - m)
        p = p / (jnp.sum(p, axis=-1, keepdims=True) + jnp.exp(sink - m))
    else:
        p = jax.nn.softmax(s, axis=-1)
    return jnp.einsum('bhgqs,bshd->bqhgd', p.astype(v.dtype), v)


def prompt_band(q, k, v, n_prev, bias_table, sinks):
    b, s_len, kvh, g, dh = q.shape
    nc = s_len // CHUNK
    pad = n_prev * CHUNK
    span = pad + CHUNK
    kp = jnp.pad(k, ((0, 0), (pad, 0), (0, 0), (0, 0)))
    vp = jnp.pad(v, ((0, 0), (pad, 0), (0, 0), (0, 0)))
    qc = jnp.moveaxis(q.reshape(b, nc, CHUNK, kvh, g, dh), 1, 0)

    def one_chunk(args):
        c, qb = args
        start = c * CHUNK
        kb = lax.dynamic_slice_in_dim(kp, start, span, axis=1)
        vb = lax.dynamic_slice_in_dim(vp, start, span, axis=1)
        q_pos = start + jnp.arange(CHUNK, dtype=jnp.int32)
        k_pos = start - pad + jnp.arange(span, dtype=jnp.int32)
        return attend(qb, kb, vb, q_pos, k_pos, k_pos >= 0, bias_table, sinks)

    out = lax.map(one_chunk, (jnp.arange(nc, dtype=jnp.int32), qc))
    return jnp.moveaxis(out, 0, 1).reshape(b, s_len, kvh, g, dh)


def ab_project(h, pos, w_in):
    b, t, _ = h.shape
    qa, ka, va, ga, qb, kb, vb, gb = _split_cols(h @ w_in, PROJ_SIZES)
    qa = rope(qa.reshape(b, t, H_A, HEAD_DIM), pos).reshape(b, t, KV_A, H_A // KV_A, HEAD_DIM)
    ka = rope(ka.reshape(b, t, KV_A, HEAD_DIM), pos)
    va = va.reshape(b, t, KV_A, HEAD_DIM)
    qb = qb.reshape(b, t, H_B, 1, HEAD_DIM)
    kb = kb.reshape(b, t, H_B, HEAD_DIM)
    vb = vb.reshape(b, t, H_B, HEAD_DIM)
    return qa, ka, va, ga, qb, kb, vb, gb


def ab_output(oa, ob, ga, gb, w_out):
    b, t = oa.shape[0], oa.shape[1]
    y = jnp.concatenate([oa.reshape(b, t, -1) * jax.nn.silu(ga),
                         ob.reshape(b, t, -1) * jax.nn.silu(gb)], axis=-1)
    return y @ w_out


def last_rows(x, n):
    extra = max(0, n - x.shape[1])
    if extra > 0:
        x = jnp.pad(x, ((0, 0), (extra, 0)) + ((0, 0),) * (x.ndim - 2))
    return x[:, x.shape[1] - n:]


def _lin_combine(c1, c2):
    a1, b1 = c1
    a2, b2 = c2
    return a1 * a2, a2 * b1 + b2


def lru_mixer(h, conv_state, h0, w_in, conv_w, conv_b, wa, ba, wx, bx, lam, w_out):
    b, t, _ = h.shape
    xb, z = jnp.split(h @ w_in, 2, axis=-1)
    xp = jnp.concatenate([conv_state.astype(xb.dtype), xb], axis=1)
    xc = conv_b.astype(xb.dtype) + xp[:, 0:t] * conv_w[0]
    for tap in range(1, CONV_W):
        xc = xc + xp[:, tap:tap + t] * conv_w[tap]
    xr = xc.reshape(b, t, N_BLOCKS, BLOCK)
    r = jax.nn.sigmoid(jnp.einsum('btni,nij->btnj', xr, wa).reshape(b, t, D_LRU).astype(jnp.float32)
                       + ba.astype(jnp.float32))
    gi = jax.nn.sigmoid(jnp.einsum('btni,nij->btnj', xr, wx).reshape(b, t, D_LRU).astype(jnp.float32)
                        + bx.astype(jnp.float32))
    log_a = C_GATE * r * jax.nn.log_sigmoid(lam.astype(jnp.float32))
    a = jnp.exp(log_a)
    u = jnp.sqrt(-jnp.expm1(2.0 * log_a)) * (gi * xc.astype(jnp.float32))
    u = u.at[:, 0].add(a[:, 0] * h0.astype(jnp.float32))
    _, hs = lax.associative_scan(_lin_combine, (a, u), axis=1)
    y = hs.astype(h.dtype) * jax.nn.silu(z)
    return y @ w_out, hs[:, -1], xp[:, xp.shape[1] - (CONV_W - 1):]


def setup_inputs(seed: int = 0) -> dict:
    key = jax.random.key(seed)
    ks = jax.random.split(key, 24)
    wa_rows = min(WINDOW_A, PAST_LEN)
    wb_rows = min(BAND_B, PAST_LEN)
    f32 = jnp.float32
    nrm = lambda k, shape, sc: jax.random.normal(k, shape, f32) * sc
    ac = jax.random.uniform(ks[22], (N_LRU, D_LRU), f32, minval=0.9, maxval=0.999)
    a_base = ac ** (1.0 / C_GATE)
    return {
        'x_prompt': nrm(ks[0], (BATCH, SEQ, D_MODEL), 1.0),
        'x_sample': nrm(ks[1], (DEC_BATCH, DEC_SEQ, D_MODEL), 1.0),
        'cache_a_k': nrm(ks[2], (N_AB, DEC_BATCH, wa_rows, KV_A, HEAD_DIM), 1.0),
        'cache_a_v': nrm(ks[3], (N_AB, DEC_BATCH, wa_rows, KV_A, HEAD_DIM), 1.0),
        'cache_b_k': nrm(ks[4], (N_AB, DEC_BATCH, wb_rows, H_B, HEAD_DIM), 1.0),
        'cache_b_v': nrm(ks[5], (N_AB, DEC_BATCH, wb_rows, H_B, HEAD_DIM), 1.0),
        'state_c_h': nrm(ks[6], (N_LRU, DEC_BATCH, D_LRU), 0.5),
        'state_c_conv': nrm(ks[7], (N_LRU, DEC_BATCH, CONV_W - 1, D_LRU), 1.0),
        'ln_pre': 1.0 + nrm(ks[8], (DEPTH, D_MODEL), 0.05),
        'ln_post': 1.0 + nrm(ks[9], (DEPTH, D_MODEL), 0.05),
        'w_in_ab': nrm(ks[10], (N_AB, D_MODEL, IN_AB), D_MODEL ** -0.5),
        'sinks_a': nrm(ks[11], (N_AB, H_A), 0.5),
        'relpos_b': nrm(ks[12], (N_AB, H_B, 2 * MAX_REL + 1), 0.1),
        'w_out_ab': nrm(ks[13], (N_AB, MIX_AB, D_MODEL), MIX_AB ** -0.5),
        'w_in_c': nrm(ks[14], (N_LRU, D_MODEL, 2 * D_LRU), D_MODEL ** -0.5),
        'conv_c_w': nrm(ks[15], (N_LRU, CONV_W, D_LRU), CONV_W ** -0.5),
        'conv_c_b': nrm(ks[16], (N_LRU, D_LRU), 0.01),
        'gate_c_wa': nrm(ks[17], (N_LRU, N_BLOCKS, BLOCK, BLOCK), BLOCK ** -0.5),
        'gate_c_ba': nrm(ks[18], (N_LRU, D_LRU), 0.01),
        'gate_c_wx': nrm(ks[19], (N_LRU, N_BLOCKS, BLOCK, BLOCK), BLOCK ** -0.5),
        'gate_c_bx': nrm(ks[20], (N_LRU, D_LRU), 0.01),
        'lambda_c': jnp.log(a_base) - jnp.log1p(-a_base),
        'w_out_c': nrm(ks[21], (N_LRU, D_LRU, D_MODEL), D_LRU ** -0.5),
    }


def reference(x_prompt, x_sample, cache_a_k, cache_a_v, cache_b_k, cache_b_v, state_c_h, state_c_conv,
              ln_pre, ln_post, w_in_ab, sinks_a, relpos_b, w_out_ab, w_in_c, conv_c_w, conv_c_b,
              gate_c_wa, gate_c_ba, gate_c_wx, gate_c_bx, lambda_c, w_out_c):
    bp, s_len = x_prompt.shape[0], x_prompt.shape[1]
    t = x_sample.shape[1]
    wa_rows = cache_a_k.shape[2]
    wb_rows = cache_b_k.shape[2]
    pos_p = jnp.arange(s_len, dtype=jnp.int32)
    pos_s = PAST_LEN + jnp.arange(t, dtype=jnp.int32)
    kpos_a = PAST_LEN - wa_rows + jnp.arange(wa_rows + t, dtype=jnp.int32)
    kpos_b = PAST_LEN - wb_rows + jnp.arange(wb_rows + t, dtype=jnp.int32)
    nak_p, nav_p, nbk_p, nbv_p, nch_p, ncc_p = [], [], [], [], [], []
    nak_s, nav_s, nbk_s, nbv_s, nch_s, ncc_s = [], [], [], [], [], []
    yp, ys = x_prompt, x_sample
    for l in range(DEPTH):
        j = l // 2
        hp = rms_norm(yp, ln_pre[l])
        hs = rms_norm(ys, ln_pre[l])
        if l % 2 == 0:
            qa, ka, va, ga, qb, kb, vb, gb = ab_project(hp, pos_p, w_in_ab[j])
            oa = prompt_band(qa, ka, va, N_PREV_A, None, sinks_a[j])
            ob = prompt_band(qb, kb, vb, N_PREV_B, relpos_b[j], None)
            mp = ab_output(oa, ob, ga, gb, w_out_ab[j])
            nak_p.append(last_rows(ka, wa_rows))
            nav_p.append(last_rows(va, wa_rows))
            nbk_p.append(last_rows(kb, wb_rows))
            nbv_p.append(last_rows(vb, wb_rows))
            qa, ka, va, ga, qb, kb, vb, gb = ab_project(hs, pos_s, w_in_ab[j])
            ka = jnp.concatenate([cache_a_k[j].astype(ka.dtype), ka], axis=1)
            va = jnp.concatenate([cache_a_v[j].astype(va.dtype), va], axis=1)
            kb = jnp.concatenate([cache_b_k[j].astype(kb.dtype), kb], axis=1)
            vb = jnp.concatenate([cache_b_v[j].astype(vb.dtype), vb], axis=1)
            oa = attend(qa, ka, va, pos_s, kpos_a, None, None, sinks_a[j])
            ob = attend(qb, kb, vb, pos_s, kpos_b, None, relpos_b[j], None)
            ms = ab_output(oa, ob, ga, gb, w_out_ab[j])
            nak_s.append(ka[:, t:])
            nav_s.append(va[:, t:])
            nbk_s.append(kb[:, t:])
            nbv_s.append(vb[:, t:])
        else:
            mp, h_last, conv_last = lru_mixer(
                hp, jnp.zeros((bp, CONV_W - 1, D_LRU), hp.dtype), jnp.zeros((bp, D_LRU), jnp.float32),
                w_in_c[j], conv_c_w[j], conv_c_b[j], gate_c_wa[j], gate_c_ba[j], gate_c_wx[j], gate_c_bx[j],
                lambda_c[j], w_out_c[j])
            nch_p.append(h_last)
            ncc_p.append(conv_last)
            ms, h_last, conv_last = lru_mixer(
                hs, state_c_conv[j], state_c_h[j],
                w_in_c[j], conv_c_w[j], conv_c_b[j], gate_c_wa[j], gate_c_ba[j], gate_c_wx[j], gate_c_bx[j],
                lambda_c[j], w_out_c[j])
            nch_s.append(h_last)
            ncc_s.append(conv_last)
        yp = yp + rms_norm(mp, ln_post[l])
        ys = ys + rms_norm(ms, ln_post[l])
    return (yp, ys,
            jnp.stack(nak_p), jnp.stack(nav_p), jnp.stack(nbk_p), jnp.stack(nbv_p),
            jnp.stack(nch_p), jnp.stack(ncc_p),
            jnp.stack(nak_s), jnp.stack(nav_s), jnp.stack(nbk_s), jnp.stack(nbv_s),
            jnp.stack(nch_s), jnp.stack(ncc_s))
```

```python
import numpy as np
from contextlib import ExitStack
import concourse.bass as bass
import concourse.mybir as mybir
from concourse.bass_utils import run_bass_kernel_spmd

F32 = mybir.dt.float32
BF16 = mybir.dt.bfloat16
AF = mybir.ActivationFunctionType
ALU = mybir.AluOpType

NCORES = 8
D = 1024
SEQ = 2048
NT = SEQ // 128
TABL = 800
EPS = 1e-6
NEG = -30000.0
N_DMA_SEMS = 24
NTILES_DEBUG = None
PIPELINE = True
DAG = True
DAG_DELTA = 0.15
KEEPALIVE = True
KA_MIN_GAP = 0.6
KA_FILL = 0.5
KA_COST = 0.11
DAG_ORIG = False


class Buf:
    __slots__ = ("name", "w", "r", "excl")

    ALL = []

    def __init__(self, name, excl=False):
        Buf.ALL.append(self)
        self.name = name
        self.excl = excl
        self.w = None
        self.r = []


class View:
    def __init__(self, t, bufs):
        self.t = t
        self.bufs = bufs

    def __getitem__(self, k):
        return self.t[k]


class Eng:
    def __init__(self, name, h, sem):
        self.name, self.h, self.sem = name, h, sem
        self.count = 0
        self.waited = {}
        self.pend_r, self.pend_w = [], []


class _Proxy:
    def __init__(self):
        self.call = None

    def __getattr__(self, name):
        def f(*a, **k):
            assert self.call is None
            self.call = (name, a, k)
            return None
        return f


def _replay(call):
    name, a, k = call
    return lambda h: getattr(h, name)(*a, **k)


PSEUDO = ("acq", "rel", "mark", "wait")
DEF_COST = {"pe": 0.11, "act": 0.5, "dve": 0.45, "pool": 0.6, "sp": 0.1}
HOP = 0.1
PRIO = [0.0, 0.0, 0.0, 0.0, 0.0]
COST_SCALE = {"pe": 1.0, "act": 1.0, "dve": 1.0, "pool": 1.0, "sp": 1.0}


class Sched:
    def __init__(self, nc, es):
        self.nc = nc
        self.es = es
        self.nsw = 0
        self.E = {}
        for name, h in (("pe", nc.tensor), ("act", nc.scalar), ("dve", nc.vector),
                        ("pool", nc.gpsimd), ("sp", nc.sync)):
            self.E[name] = Eng(name, h, es.enter_context(nc.semaphore("s_" + name)))
        self.dsem = [es.enter_context(nc.semaphore(f"d{i}")) for i in range(N_DMA_SEMS)]
        self.dval = [0] * N_DMA_SEMS
        self.dnext = 0
        self.store_toks = []
        self.rec = None
        self.dry = False
        self.dummy_items = []
        self.n_dummy = 0
        self.trace = None
        self.free_t = {n: 0.0 for n in self.E}
        self.tok_t = {}
        self.busy = {}

    @staticmethod
    def _bufs(lst):
        out = []
        for x in lst:
            if isinstance(x, Buf):
                out.append(x)
            else:
                out.extend(x.bufs)
        return out

    def _wait(self, eng, tok):
        sem, val, _ = tok
        key = id(sem)
        if eng.waited.get(key, 0) >= val:
            return
        eng.h.wait_ge(sem, val)
        eng.waited[key] = val

    def _dep_toks(self, en, rb, wb, is_dma):
        toks = []
        for b in rb:
            if b.w is not None:
                if b.w[2] == en and en == "pe" and not is_dma:
                    continue
                toks.append(b.w)
        for b in wb:
            if b.w is not None:
                if not (b.w[2] == en and en == "pe" and not is_dma):
                    toks.append(b.w)
            for r in b.r:
                if r[2] == en and en == "pe" and not is_dma:
                    continue
                toks.append(r)
        return toks

    def _split(self, rd, wr):
        rb, wb = self._bufs(rd), self._bufs(wr)
        wb = wb + [b for b in rb if b.excl]
        rb = [b for b in rb if not b.excl]
        return rb, wb

    def _ready_t(self, toks):
        t = 0.0
        for tok in toks:
            t = max(t, self.tok_t.get((id(tok[0]), tok[1]), 0.0) + HOP)
        return t

    def peek(self, item):
        kind, en, fn, rd, wr, defer, c, kw = item
        rb, wb = self._split(rd, wr)
        toks = self._dep_toks(en, rb, wb, kind == "dma")
        return max(self.free_t[en], self._ready_t(toks))

    def acquire(self, name):
        if self.rec is not None:
            self.rec.append(("acq", name))

    def release(self, name):
        if self.rec is not None:
            self.rec.append(("rel", name))

    def mark(self, name):
        if self.rec is not None:
            self.rec.append(("mark", name))

    def wait_mark(self, name):
        if self.rec is not None:
            self.rec.append(("wait", name))

    def op(self, en, fn, rd=(), wr=(), defer=False, c=None):
        if self.rec is not None:
            p = _Proxy()
            fn(p)
            assert p.call is not None
            self.rec.append(("op", en, _replay(p.call), list(rd), list(wr), defer, c, None))
            return
        eng = self.E[en]
        rb, wb = self._split(rd, wr)
        for o in self.E.values():
            if o is not eng and (o.pend_r or o.pend_w):
                for b in wb:
                    assert all(b is not p for p in o.pend_r + o.pend_w), (en, o.name, b.name)
                for b in rb:
                    assert all(b is not p for p in o.pend_w), (en, o.name, b.name)
        toks = self._dep_toks(en, rb, wb, False)
        if not self.dry:
            for tok in toks:
                self._wait(eng, tok)
        start = max(self.free_t[en], self._ready_t(toks))
        end = start + (c if c is not None else DEF_COST[en]) * COST_SCALE[en]
        self.free_t[en] = end
        self.busy[en] = self.busy.get(en, 0.0) + end - start
        inst = None if self.dry else fn(eng.h)
        if defer:
            eng.pend_r += rb
            eng.pend_w += wb
            return
        eng.count += 1
        if not self.dry:
            inst.then_inc(eng.sem, 1)
        tok = (eng.sem, eng.count, en)
        self.tok_t[(id(eng.sem), eng.count)] = end
        rb += eng.pend_r
        wb += eng.pend_w
        eng.pend_r, eng.pend_w = [], []
        for b in wb:
            b.w = tok
            b.r = []
        for b in rb:
            b.r = [r for r in b.r if r[2] != en] + [tok]

    def dma(self, q, out, in_, rd=(), wr=(), store=False, c=None, occ=0.1, **kw):
        if self.rec is not None:
            self.rec.append(("dma", q, (out, in_, store, occ), list(rd), list(wr), False, c, kw))
            return
        eng = self.E[q]
        assert not eng.pend_r and not eng.pend_w
        rb, wb = self._bufs(rd), self._bufs(wr)
        for o in self.E.values():
            if o.pend_r or o.pend_w:
                for b in wb:
                    assert all(b is not p for p in o.pend_r + o.pend_w), ("dma", q, o.name, b.name)
                for b in rb:
                    assert all(b is not p for p in o.pend_w), ("dma", q, o.name, b.name)
        toks = self._dep_toks(q, rb, wb, True)
        if q == "pool":
            if self.dry:
                sem_, val_ = ("drysw", len(self.tok_t)), 16
            else:
                sem_, val_ = self.es.enter_context(self.nc.semaphore(f"sw{self.nsw}")), 16
                self.nsw += 1
                for tok in toks:
                    self._wait(eng, tok)
                eng.h.dma_start(out=out, in_=in_, **kw).then_inc(sem_, 16)
            tok = (sem_, val_, None)
        else:
            i = self.dnext
            self.dnext = (i + 1) % N_DMA_SEMS
            if not self.dry:
                for tok in toks:
                    self._wait(eng, tok)
                if self.dval[i]:
                    self._wait(eng, (self.dsem[i], self.dval[i], None))
            self.dval[i] += 16
            if not self.dry:
                eng.h.dma_start(out=out, in_=in_, **kw).then_inc(self.dsem[i], 16)
            tok = (self.dsem[i], self.dval[i], None)
        start = max(self.free_t[q], self._ready_t(toks))
        self.free_t[q] = start + occ
        self.tok_t[(id(tok[0]), tok[1])] = start + (c if c is not None else 3.0)
        for b in wb:
            b.w = tok
            b.r = []
        for b in rb:
            b.r.append(tok)
        if store:
            self.store_toks.append(tok)
        return tok

    def emit(self, item):
        if self.trace is not None:
            self.trace.append(item)
        kind, en, fn, rd, wr, defer, c, kw = item
        if kind == "op":
            self.op(en, fn, rd, wr, defer, c)
        else:
            out, in_, store, occ = fn
            self.dma(en, out, in_, rd, wr, store, c, occ, **kw)

    def record(self, gen):
        assert self.rec is None
        self.rec = []
        for _ in gen:
            pass
        lst, self.rec = self.rec, None
        return lst

    def run_streams(self, lists, use_sim=True):
        n = len(lists)
        idx = [0] * n
        held = {}
        marks = set()
        rr = 0
        while True:
            progressed = True
            while progressed:
                progressed = False
                for i in range(n):
                    while idx[i] < len(lists[i]) and lists[i][idx[i]][0] in PSEUDO:
                        kind, name = lists[i][idx[i]]
                        if kind == "acq":
                            if held.get(name, i) != i:
                                break
                            held[name] = i
                        elif kind == "rel":
                            assert held.get(name) == i, (name, i)
                            del held[name]
                        elif kind == "mark":
                            marks.add(name)
                        elif name not in marks:
                            break
                        idx[i] += 1
                        progressed = True
            cands = [i for i in range(n) if idx[i] < len(lists[i]) and lists[i][idx[i]][0] not in PSEUDO]
            if not cands:
                assert all(idx[i] >= len(lists[i]) for i in range(n)), ("lock deadlock", held, idx)
                return
            if use_sim:
                best = min(cands, key=lambda i: (self.peek(lists[i][idx[i]]) - (PRIO[i] if n == len(PRIO) else 0.0),
                                                 (i - rr) % n))
            else:
                best = cands[0]
            rr = (best + 1) % n
            self.emit(lists[best][idx[best]])
            idx[best] += 1

    def snapshot(self):
        return {
            "bufs": [(b, b.w, list(b.r)) for b in Buf.ALL],
            "eng": {n: (e.count, dict(e.waited), list(e.pend_r), list(e.pend_w)) for n, e in self.E.items()},
            "d": (list(self.dval), self.dnext), "free": dict(self.free_t), "tok": dict(self.tok_t),
            "busy": dict(self.busy), "stores": list(self.store_toks),
        }

    def restore(self, sn):
        for b, w, r in sn["bufs"]:
            b.w, b.r = w, list(r)
        for n, (cnt, wt, pr, pw) in sn["eng"].items():
            e = self.E[n]
            e.count, e.waited, e.pend_r, e.pend_w = cnt, dict(wt), list(pr), list(pw)
        self.dval, self.dnext = list(sn["d"][0]), sn["d"][1]
        self.free_t, self.tok_t, self.busy = dict(sn["free"]), dict(sn["tok"]), dict(sn["busy"])
        self.store_toks = list(sn["stores"])

    def run_dag(self, order, delta=0.3):
        groups, open_g = [], {}
        gof = []
        for it in order:
            kind, en = it[0], it[1]
            key = en if kind == "op" else None
            if key is not None and key in open_g:
                gi = open_g[key]
            else:
                gi = len(groups)
                groups.append({"items": [], "en": en, "cost": 0.0})
                if key is not None:
                    open_g[key] = gi
            g = groups[gi]
            g["items"].append(it)
            gof.append(gi)
            c = it[6]
            g["cost"] += (c if c is not None else DEF_COST[en]) if kind == "op" else 0.1
            if kind == "op" and not it[5]:
                open_g.pop(key, None)
        assert not open_g, "deferred ops without a covering op"
        n = len(groups)
        lastw, readers = {}, {}
        preds = [set() for _ in range(n)]
        for it, gi in zip(order, gof):
            rb, wb = self._split(it[3], it[4])
            for b in rb:
                if id(b) in lastw:
                    preds[gi].add(lastw[id(b)])
            for b in wb:
                if id(b) in lastw:
                    preds[gi].add(lastw[id(b)])
                for r in readers.get(id(b), ()):
                    preds[gi].add(r)
            for b in wb:
                lastw[id(b)] = gi
                readers[id(b)] = []
            for b in rb:
                readers.setdefault(id(b), []).append(gi)
        lastq = {}
        for gi, g in enumerate(groups):
            if g["items"][0][0] == "dma":
                q = g["en"]
                if q in lastq:
                    preds[gi].add(lastq[q])
                lastq[q] = gi
        for gi in range(n):
            preds[gi].discard(gi)
        succs = [[] for _ in range(n)]
        for gi in range(n):
            for p in preds[gi]:
                succs[p].append(gi)
        rem = [0.0] * n
        for gi in range(n - 1, -1, -1):
            m = 0.0
            for s_ in succs[gi]:
                m = max(m, rem[s_])
            rem[gi] = m + groups[gi]["cost"] + HOP
        npred = [len(p) for p in preds]
        started = set()
        ready = [gi for gi in range(n) if npred[gi] == 0]
        done = 0
        while ready:
            starts = [(self.peek(groups[gi]["items"][0]), gi) for gi in ready]
            tmin = min(s_ for s_, _ in starts)
            cands = [gi for s_, gi in starts if s_ <= tmin + delta]
            best = min(ready) if DAG_ORIG else max(cands, key=lambda gi: (rem[gi], -gi))
            ready.remove(best)
            if KEEPALIVE and groups[best]["en"] == "pe" and self.dummy_items:
                gap = self.peek(groups[best]["items"][0]) - self.free_t["pe"]
                if gap > KA_MIN_GAP:
                    budget = (gap - 0.15) * KA_FILL
                    usable = [d for d in self.dummy_items if id(self._bufs(d[4])[0]) in started]
                    while budget > KA_COST and usable:
                        cand = min(usable, key=self.peek)
                        if self.peek(cand) > self.free_t["pe"] + 0.05:
                            break
                        self.emit(cand)
                        self.n_dummy += 1
                        budget -= KA_COST
            for it in groups[best]["items"]:
                self.emit(it)
                if it[0] == "op" and it[1] == "pe":
                    for b in self._bufs(it[4]):
                        started.add(id(b))
            done += 1
            for s_ in succs[best]:
                npred[s_] -= 1
                if npred[s_] == 0:
                    ready.append(s_)
        assert done == n, (done, n)

    def finish(self):
        eng = self.E["sp"]
        for tok in self.store_toks:
            self._wait(eng, tok)
        for en in ("pe", "act", "dve", "pool"):
            e = self.E[en]
            assert not e.pend_r and not e.pend_w, en
            if e.count:
                self._wait(eng, (e.sem, e.count, en))


def build_program():
    nc = bass.Bass("TRN2", target_bir_lowering=False)

    def din(name, shape):
        return nc.dram_tensor(name, list(shape), F32, kind="ExternalInput").ap()

    def dout(name, shape):
        return nc.dram_tensor(name, list(shape), F32, kind="ExternalOutput").ap()

    xp = din("xp", [2, SEQ, D])
    xs = din("xs", [128, D])
    cak = din("cak", [4, 128, 128])
    cav = din("cav", [4, 128, 128])
    cbk = din("cbk", [4, 512, 512])
    cbv = din("cbv", [4, 512, 512])
    st4 = din("st4", [4, 4, D])
    vecs = din("vecs", [80, 128])
    lnpost = din("lnpost", [2, D])
    w_in = din("w_in", [D, 3328])
    w_out0 = din("w_out0", [D, D])
    w_inc = din("w_inc", [D, 2048])
    w_out1 = din("w_out1", [D, D])
    sinks = din("sinks", [8])
    tab = din("tab", [8, TABL])
    gwa = din("gwa", [16, 64, 64])
    gwx = din("gwx", [16, 64, 64])
    ropec = din("ropec", [128, 17, 32])
    ropes = din("ropes", [128, 17, 32])
    cmask = din("cmask", [128, 128])

    o_yp = dout("o_yp", [2, SEQ, D])
    o_ys = dout("o_ys", [128, D])
    o_akp = dout("o_akp", [2, 128, 128])
    o_avp = dout("o_avp", [2, 128, 128])
    o_bkp = dout("o_bkp", [2, 512, 512])
    o_bvp = dout("o_bvp", [2, 512, 512])
    o_chp = dout("o_chp", [2, D])
    o_ccp = dout("o_ccp", [2, 3, D])
    o_aks = dout("o_aks", [4, 128, 128])
    o_avs = dout("o_avs", [4, 128, 128])
    o_bks = dout("o_bks", [4, 512, 512])
    o_bvs = dout("o_bvs", [4, 512, 512])
    o_chs = dout("o_chs", [4, D])
    o_ccs = dout("o_ccs", [4, 3, D])

    es = ExitStack()
    with es:
        S = Sched(nc, es)
        op, dma = S.op, S.dma

        def sb(name, shape, dt=F32, nb=1):
            t = es.enter_context(nc.sbuf_tensor(name, list(shape), dt))
            return View(t, [Buf(name + str(i)) for i in range(nb)])

        def alias(v, dt, shape, bufs=None, off=0):
            t = v.t[:]
            if dt != F32:
                t = t.bitcast(dt)
            esz = 2 if dt == BF16 else 4
            n = int(np.prod(shape[1:]))
            o = off // esz
            t = t[:, o:o + n]
            if len(shape) == 3:
                t = t.rearrange("p (a b) -> p a b", a=shape[1])
            elif len(shape) == 4:
                t = t.rearrange("p (a b c) -> p a b c", a=shape[1], b=shape[2])
            return View(t, v.bufs if bufs is None else bufs)

        banks = [View(es.enter_context(nc.psum_tensor(f"bk{i}", [128, 512], F32)), [Buf(f"bk{i}", excl=True)])
                 for i in range(8)]
        IP = banks[0:2]
        TR = banks[2:4]
        ST = banks[4:6]
        PV = banks[6:8]

        def bk_bf(b):
            return b.t[:].bitcast(BF16)

        Win = sb("Win", [128, 8, 3328], BF16, nb=1)
        Wout0 = sb("Wout0", [128, 8, D], BF16)
        Winc = sb("Winc", [128, 8, 2048], BF16)
        Wout1 = sb("Wout1", [128, 8, D], BF16)
        BDa = sb("BDa", [128, 8, 128], BF16)
        BDx = sb("BDx", [128, 8, 128], BF16)
        GPOST = [sb(f"GPOST{l}", [128, D]) for l in range(2)]
        VT = sb("VT", [128, 80])
        ROPC = sb("ROPC", [128, 2, 32], nb=2)
        ROPS = sb("ROPS", [128, 2, 32], nb=2)
        EXS = sb("EXS", [128, 8])
        CLOG = sb("CLOG", [128, 8])
        IDB = sb("IDB", [128, 128], BF16)
        IDF = sb("IDF", [128, 128])
        JB = sb("JB", [128, 128], BF16)
        MSK = sb("MSK", [128, 128], BF16)
        BT = sb("BT", [128, 8, 4, 128], BF16)
        BSN = sb("BSN", [128, 8, 128], BF16)
        HALF = sb("HALF", [128, 1])

        X = [sb(f"X{i}", [128, D]) for i in range(2)]
        SSL = [sb(f"SS{l}", [128, 4]) for l in range(2)]
        hTL = [sb(f"hT{l}", [128, 8, 128], BF16) for l in range(2)]
        G1 = sb("G1", [128, 1024], F32, nb=4)
        G2 = sb("G2", [128, 1024], F32, nb=2)
        G3 = sb("G3", [128, 1024], F32, nb=2)
        G4 = sb("G4", [128, 1024], F32, nb=2)
        G5 = sb("G5", [128, 960], F32, nb=5)
        G6 = sb("G6", [128, 1024], F32, nb=2)
        G7 = sb("G7", [128, 1024], F32, nb=4)
        PTA = [alias(G1, BF16, [128, 4, 128], [G1.bufs[i]], off=1024 * i) for i in range(4)]
        T1 = alias(G2, F32, [128, 512], [G2.bufs[0]], off=0)
        T2 = alias(G2, F32, [128, 512], [G2.bufs[1]], off=2048)
        CBKb = alias(G2, BF16, [128, 4, 512])
        GTMP = alias(G3, F32, [128, 512], [G3.bufs[0]], off=0)
        GS = alias(G3, BF16, [128, 1024], [G3.bufs[1]], off=2048)
        Y = alias(G4, BF16, [128, 1024], [G4.bufs[0]] + [Buf(f"Yq{i}") for i in range(4)], off=0)
        YT = alias(G4, BF16, [128, 8, 128], [G4.bufs[1]], off=2048)
        QA = alias(G5, BF16, [128, 512], [G5.bufs[0]], off=0)
        KAf = alias(G5, F32, [128, 128], [G5.bufs[1]], off=1024)
        KAb = alias(G5, BF16, [128, 128], [G5.bufs[2]], off=1536)
        QB = alias(G5, BF16, [128, 512], [G5.bufs[3]], off=1792)
        KB = alias(G5, BF16, [128, 512], [G5.bufs[4]], off=2816)
        QAT = alias(G6, BF16, [128, 2, 4, 128], [G6.bufs[0]], off=0)
        QBT = alias(G6, BF16, [128, 4, 2, 128], [G6.bufs[1]], off=2048)
        PTB = [alias(G7, BF16, [128, 4, 128], [G7.bufs[i]], off=1024 * i) for i in range(4)]
        XN = alias(G4, F32, [128, D])
        KBf = alias(G3, F32, [128, 512], [G3.bufs[0]], off=0)
        VBf = alias(G2, F32, [128, 512], [G2.bufs[0]], off=0)
        STG = alias(G4, F32, [128, 8, 128])
        VAf = alias(G7, F32, [128, 128], [G7.bufs[0]], off=0)
        KAT = sb("KAT", [128, 2, 128], BF16, nb=2)
        KBT = sb("KBT", [128, 4, 5, 128], BF16, nb=5)
        VA = sb("VA", [128, 2, 2, 65], BF16, nb=2)
        VB = sb("VB", [128, 5, 8, 65], BF16, nb=5)
        CAKb = alias(G4, BF16, [128, 128], [G4.bufs[0]], off=0)
        PVT2 = [alias(G2, F32, [128, 4, 64], [G2.bufs[i]], off=2048 * i) for i in range(2)]
        DEN = sb("DEN", [128, 8], nb=2)
        class PSet:
            pass
        PS = []
        for i in range(2):
            p = PSet()
            p.XB = sb(f"pXB{i}", [128, 2 * 140])
            p.XBp = View(p.XB.t[:, 0:2 * 131].rearrange("p (c t) -> p c t", c=2), p.XB.bufs)
            p.XBs = View(p.XB.t[:].rearrange("p (c s t) -> p c s t", c=2, s=4), p.XB.bufs)
            p.XC = sb(f"pXC{i}", [128, 2, 128])
            p.XCb = sb(f"pXCb{i}", [128, 2, 128], BF16)
            p.LAB = sb(f"pLAB{i}", [128, 4, 128])
            p.LC = sb(f"pLC{i}", [128, 2, 128])
            p.HS = sb(f"pHS{i}", [128, 2, 128])
            p.ZS = sb(f"pZS{i}", [128, 2, 128], BF16)
            PS.append(p)
        pLAB = PS[0].LAB
        Y1T = View(BSN.t, [Buf(f"Y1T{q}") for q in range(4)] + BSN.bufs)
        HIST = sb("HIST", [128, 8, 3], nb=4)
        HST = sb("HST", [128, 8, 1], nb=4)
        ST4 = sb("ST4", [128, 8, 4, 4], nb=4)
        INIT = sb("INIT", [128, 8, 4, 4])
        S4 = View(pLAB.t[:].rearrange("p a t -> p (a t)"), pLAB.bufs)
        O4 = S4

        def ring(v, i):
            return View(v.t, [v.bufs[i]])

        op("pool", lambda e: e.memset(IDF[:], 1.0), wr=[IDF])
        op("pool", lambda e: e.affine_select(out=IDF[:], in_=IDF[:], pattern=[[-1, 128]],
                                             compare_op=ALU.is_equal, fill=0.0, base=0,
                                             channel_multiplier=1), rd=[IDF], wr=[IDF])
        op("pool", lambda e: e.tensor_copy(out=IDB[:], in_=IDF[:]), rd=[IDF], wr=[IDB])
        op("pool", lambda e: e.memset(T1[:, 0:128], 1.0), wr=[T1])
        op("pool", lambda e: e.affine_select(out=T1[:, 0:128], in_=T1[:, 0:128], pattern=[[1, 128]],
                                             compare_op=ALU.is_equal, fill=0.0, base=-127,
                                             channel_multiplier=1), rd=[T1], wr=[T1])
        op("pool", lambda e: e.tensor_copy(out=JB[:], in_=T1[:, 0:128]), rd=[T1], wr=[JB])
        op("pool", lambda e: e.memset(HALF[:], -0.5), wr=[HALF])
        dma("pool", MSK[:], cmask, wr=[MSK])
        op("pool", lambda e: e.memset(VA[:], 1.0), wr=[VA])
        op("pool", lambda e: e.memset(VB[:], 1.0), wr=[VB])
        op("pool", lambda e: e.memset(QAT[:], 0.0), wr=[QAT])
        op("pool", lambda e: e.memset(QBT[:], 0.0), wr=[QBT])
        op("pool", lambda e: e.memset(BDa[:], 0.0), wr=[BDa])
        op("pool", lambda e: e.memset(BDx[:], 0.0), wr=[BDx])
        dma("sp", T2[0:80, 0:128], vecs, wr=[T2])
        dma("sp", EXS[:], sinks.partition_broadcast(128), wr=[EXS])
        for l in range(2):
            dma("sp", GPOST[l][:], lnpost[l].partition_broadcast(128), wr=[GPOST[l]])
        op("pe", lambda e: e.transpose(out=TR[0][:, 0:80], in_=T2[0:80, 0:128], identity=IDF[0:80, 0:80]),
           rd=[T2, IDF], wr=[TR[0]])
        op("dve", lambda e: e.tensor_copy(out=VT[:], in_=TR[0][:, 0:80]), rd=[TR[0]], wr=[VT])
        VTv = View(VT.t[:].rearrange("p (v c) -> p v c", v=10), VT.bufs)
        op("act", lambda e: e.activation(out=EXS[:], in_=EXS[:], func=AF.Exp), rd=[EXS], wr=[EXS])
        op("act", lambda e: e.activation(out=CLOG[:], in_=VTv[:, 7, :], func=AF.Exp, scale=-1.0),
           rd=[VT], wr=[CLOG])
        op("act", lambda e: e.activation(out=CLOG[:], in_=CLOG[:], func=AF.Ln, bias=1.0, scale=1.0),
           rd=[CLOG], wr=[CLOG])
        op("dve", lambda e: e.tensor_scalar(out=CLOG[:], in0=CLOG[:], scalar1=-8.0, scalar2=None,
                                            op0=ALU.mult), rd=[CLOG], wr=[CLOG])

        def toep(off, npart, nq):
            return bass.AP(tensor=tab.tensor, offset=off, ap=[[1, npart], [TABL, 8], [1, nq]])
        for i, dt_ in enumerate((0, 1, 2, 4)):
            dma("sp", STG[:], toep(128 * dt_ + 1, 128, 128), wr=[STG])
            op("dve", lambda e, i=i: e.tensor_scalar(out=BT[:, :, i, :], in0=STG[:], scalar1=8.0,
                                                     scalar2=None, op0=ALU.mult), rd=[STG], wr=[BT])
        op("pool", lambda e: e.memset(BT[0:64, :, 0, 0:64], NEG), wr=[BT])
        op("pool", lambda e: e.memset(BT[64:128, :, 3, 64:128], NEG), wr=[BT])
        op("dve", lambda e: e.memset(STG[:], NEG / 8.0), wr=[STG])
        for s in range(4):
            dma("sp", STG[96 - 32 * s:128 - 32 * s, :, 32 * s:32 * s + 32], toep(97, 32, 32), wr=[STG])
        op("dve", lambda e: e.tensor_scalar(out=BSN[:], in0=STG[:], scalar1=8.0, scalar2=None,
                                            op0=ALU.mult), rd=[STG], wr=[BSN])

        def wload(dst, src, ncols):
            srcv = src.rearrange("(k p) n -> p k n", p=128)
            t_ = ncols / 512.0 * 1.35
            for k in range(8):
                dma("pool", dst[:, k, :], srcv[:, k, :], wr=[dst], max_dma_last_dim=4096, c=t_ + 2.0, occ=t_)
        wload(Win, w_in, 3328)

        def gen_weights():
            wload(Wout0, w_out0, D)
            wload(Winc, w_inc, 2048)
            for (bd, gw) in ((BDa, gwa), (BDx, gwx)):
                gv = gw.rearrange("(c two) i j -> two i c j", two=2)
                dma("pool", bd[0:64, :, 0:64], gv[0], wr=[bd], occ=1.0)
                dma("pool", bd[64:128, :, 64:128], gv[1], wr=[bd], occ=1.0)
            wload(Wout1, w_out1, D)
            yield

        def rstd_from(SS, n):
            if n == 2:
                op("dve", lambda e: e.tensor_tensor(out=SS[:, 2:3], in0=SS[:, 0:1], in1=SS[:, 1:2],
                                                    op=ALU.add), rd=[SS], wr=[SS])
                src = SS[:, 2:3]
            else:
                src = SS[:, 0:1]
            op("dve", lambda e: e.tensor_scalar(out=SS[:, 2:3], in0=src, scalar1=1.0 / D, scalar2=EPS,
                                                op0=ALU.mult, op1=ALU.add), rd=[SS], wr=[SS])
            op("pool", lambda e: e.tensor_tensor(out=SS[:, 3:4], in0=SS[:, 2:3], in1=HALF[:], op=ALU.pow),
               rd=[SS, HALF], wr=[SS])

        def prenorm(Xt, layer):
            SS = SSL[layer]
            hT = hTL[layer]
            if layer == 0:
                op("act", lambda e: e.activation(out=XN[:], in_=Xt[:], func=AF.Square, accum_out=SS[:, 0:1]),
                   rd=[Xt], wr=[XN, SS], c=0.85)
                rstd_from(SS, 1)
                op("dve", lambda e: e.tensor_scalar(out=XN[:], in0=Xt[:], scalar1=SS[:, 3:4], scalar2=None,
                                                    op0=ALU.mult), rd=[Xt, SS], wr=[XN], c=1.2)
            else:
                tmp = pLAB[:].rearrange("p a t -> p (a t)")
                for half in range(2):
                    op("act", lambda e, half=half: e.activation(
                        out=tmp, in_=Xt[:, half * 512:(half + 1) * 512], func=AF.Square,
                        accum_out=SS[:, half:half + 1]), rd=[Xt], wr=[pLAB, SS], c=0.6)
                rstd_from(SS, 2)
            for half in range(2):
                bk = TR[half]
                if layer == 1:
                    op("dve", lambda e, half=half: e.tensor_scalar(
                        out=tmp, in0=Xt[:, half * 512:(half + 1) * 512], scalar1=SS[:, 3:4], scalar2=None,
                        op0=ALU.mult), rd=[Xt, SS], wr=[pLAB], c=0.7)
                S.acquire(f"TR{half}")
                for k4 in range(4):
                    k = half * 4 + k4
                    if layer == 0:
                        src, srcv = XN[:, k * 128:(k + 1) * 128], XN
                    else:
                        src, srcv = tmp[:, k4 * 128:(k4 + 1) * 128], pLAB
                    op("pe", lambda e, k4=k4, bk=bk, src=src: e.transpose(
                        out=bk[:, k4 * 128:(k4 + 1) * 128], in_=src, identity=IDF[:]),
                        rd=[srcv, IDF], wr=[bk], defer=(k4 < 3), c=0.15)
                g = VTv[:, 8 + layer, half * 4:half * 4 + 4].unsqueeze(2).to_broadcast([128, 4, 128])
                op("dve", lambda e, bk=bk, half=half, g=g: e.tensor_tensor(
                    out=hT[:, half * 4:half * 4 + 4, :], in0=bk[:, :].rearrange("p (a b) -> p a b", a=4),
                    in1=g, op=ALU.mult), rd=[bk, VT], wr=[hT], c=0.7)
                S.release(f"TR{half}")

        def postnorm(Xt, layer):
            SS = SSL[layer]
            if layer == 0:
                tmp, tmpv = GTMP[:], GTMP
            else:
                tmp, tmpv = pLAB[:].rearrange("p a t -> p (a t)"), pLAB
            for n in range(2):
                op("act", lambda e, n=n: e.activation(out=tmp, in_=IP[n][:, :], func=AF.Square,
                                                      accum_out=SS[:, n:n + 1]), rd=[IP[n]], wr=[tmpv, SS], c=0.6)
            rstd_from(SS, 2)
            for n in range(2):
                op("dve", lambda e, n=n: e.scalar_tensor_tensor(
                    out=tmp, in0=IP[n][:, :], scalar=SS[:, 3:4], in1=GPOST[layer][:, n * 512:(n + 1) * 512],
                    op0=ALU.mult, op1=ALU.mult), rd=[IP[n], SS, GPOST[layer]], wr=[tmpv], c=0.7)
                op("pool", lambda e, n=n: e.tensor_tensor(
                    out=Xt[:, n * 512:(n + 1) * 512], in0=Xt[:, n * 512:(n + 1) * 512], in1=tmp, op=ALU.add),
                    rd=[Xt, tmpv], wr=[Xt], c=0.8)

        def sigmoid_inplace(ap, view):
            op("act", lambda e: e.activation(out=ap, in_=ap, func=AF.Exp, scale=-1.0), rd=[view], wr=[view])
            op("act", lambda e: e.activation(out=ap, in_=ap, func=AF.Ln, bias=1.0, scale=1.0), rd=[view], wr=[view])
            op("act", lambda e: e.activation(out=ap, in_=ap, func=AF.Exp, scale=-1.0), rd=[view], wr=[view])

        def silu_from(bank_ap, out_ap, out_view, bank, tmp, tmpv):
            op("act", lambda e: e.activation(out=tmp, in_=bank_ap, func=AF.Exp, scale=-1.0),
               rd=[bank], wr=[tmpv])
            op("act", lambda e: e.activation(out=tmp, in_=tmp, func=AF.Ln, bias=1.0, scale=1.0),
               rd=[tmpv], wr=[tmpv])
            op("act", lambda e: e.activation(out=tmp, in_=tmp, func=AF.Exp, scale=-1.0),
               rd=[tmpv], wr=[tmpv])
            op("dve", lambda e: e.tensor_tensor(out=out_ap, in0=bank_ap, in1=tmp, op=ALU.mult),
               rd=[bank, tmpv], wr=[out_view])

        def rope(bank, c0, nh, ti, out_ap, out_view):
            n = nh * 64
            src = bank[:, c0:c0 + n].rearrange("p (h t j) -> p h t j", h=nh, t=2)
            t1 = T1[:, 0:n].rearrange("p (h t j) -> p h t j", h=nh, t=2)
            t2 = T2[:, 0:n].rearrange("p (h t j) -> p h t j", h=nh, t=2)
            o = out_ap.rearrange("p (h t j) -> p h t j", h=nh, t=2)
            rcv, rsv = ring(ROPC, ti), ring(ROPS, ti)
            cb = ROPC[:, ti, :].unsqueeze(1).to_broadcast([128, nh, 32])
            sn = ROPS[:, ti, :].unsqueeze(1).to_broadcast([128, nh, 32])
            for t in range(2):
                op("dve", lambda e, t=t: e.tensor_tensor(out=t1[:, :, t, :], in0=src[:, :, t, :], in1=cb,
                                                         op=ALU.mult), rd=[bank, rcv], wr=[T1], defer=(t == 0))
            for t in range(2):
                op("dve", lambda e, t=t: e.tensor_tensor(out=t2[:, :, t, :], in0=src[:, :, t, :], in1=sn,
                                                         op=ALU.mult), rd=[bank, rsv], wr=[T2], defer=(t == 0))
            op("pool", lambda e: e.tensor_tensor(out=o[:, :, 0, :], in0=t1[:, :, 0, :], in1=t2[:, :, 1, :],
                                                 op=ALU.subtract), rd=[T1, T2], wr=[out_view])
            op("pool", lambda e: e.tensor_tensor(out=o[:, :, 1, :], in0=t1[:, :, 1, :], in1=t2[:, :, 0, :],
                                                 op=ALU.add), rd=[T1, T2], wr=[out_view])

        def transposes_bf(src_view, src_ap_fn, n, bank):
            bb = bk_bf(bank)
            for i in range(n):
                op("pe", lambda e, i=i: e.transpose(out=bb[:, i * 128:(i + 1) * 128], in_=src_ap_fn(i),
                                                    identity=IDB[:]),
                   rd=[src_view, IDB], wr=[bank], defer=(i < n - 1))
            return bb

        state = {"ptb": 0, "st": 0}

        def next_st():
            b = ST[state["st"]]
            state["st"] ^= 1
            return b

        def load_x(i):
            kind, seq, m = tiles[i]
            Xt = X[i % 2]
            ti = 0 if kind == "s" else 1 + m
            dma("sp", ROPC[:, i % 2, :], ropec[:, ti, :], wr=[ring(ROPC, i % 2)])
            dma("sp", ROPS[:, i % 2, :], ropes[:, ti, :], wr=[ring(ROPS, i % 2)])
            if kind == "s":
                dma("sp", Xt[:], xs, wr=[Xt])
            else:
                dma("sp", Xt[:], xp[seq, m * 128:(m + 1) * 128, :], wr=[Xt])

        allq = slice(0, 128)

        def run_steps(steps):
            n = len(steps)
            if n == 0:
                return
            steps[0]["qk"]()
            for i in range(n):
                steps[i]["exp"]()
                hoist = i + 1 < n and not steps[i + 1]["nohoist"]
                if hoist:
                    steps[i + 1]["qk"]()
                steps[i]["pv"]()
                if i + 1 < n and not hoist:
                    steps[i + 1]["qk"]()
                yield

        def gen_attn(tno, kind, seq, m, halves):
            sample = kind == "s"
            cnt = {"st": 0, "ptb": 0}

            def pick_st(g):
                if len(halves) == 1:
                    return ST[halves[0]]
                cnt["st"] ^= 1
                return ST[cnt["st"]]

            def pick_ptb(hg):
                if len(halves) == 1:
                    cnt["ptb"] ^= 1
                    return PTB[2 * halves[0] + cnt["ptb"]]
                cnt["ptb"] = (cnt["ptb"] + 1) % 4
                return PTB[cnt["ptb"]]

            pvfirst = {"A": [True, True], "B": [True, True]}

            def pv_mm(kind_, bankidx, col, lhsT, rhs, rd, last):
                bank = PV[bankidx]
                st_ = pvfirst[kind_][bankidx]
                pvfirst[kind_][bankidx] = False
                op("pe", lambda e: e.matmul(bank[:, col:col + 65], lhsT=lhsT, rhs=rhs, start=st_, stop=True,
                                            skip_group_check=True),
                   rd=rd, wr=[bank], defer=not last)

            def a_step(slot, g, qsl, nq, mask_block, use_msk, last, parity, pre=None):
                kar_ = ring(KAT, slot)
                var_ = ring(VA, slot)
                pt = PTA[2 * g + parity]
                box = {}

                def qk():
                    if pre is not None:
                        pre()
                    stb = pick_st(g)
                    box["stb"] = stb
                    stv = stb[:, :].rearrange("p (a b) -> p a b", a=4)
                    so = stv[:, :, qsl] if nq == 128 else stb[:, 0:4 * nq].rearrange("p (a b) -> p a b", a=4)
                    op("pe", lambda e: e.matmul(so, lhsT=KAT[:, slot, :], rhs=QAT[:, g, :, qsl],
                                                start=True, stop=True, skip_group_check=True),
                       rd=[kar_, QAT], wr=[stb])

                def ex():
                    stb = box["stb"]
                    stv = stb[:, :].rearrange("p (a b) -> p a b", a=4)
                    if nq < 128:
                        op("pool", lambda e: e.memset(pt[:], 0.0), wr=[pt])
                    if mask_block == "prev":
                        op("act", lambda e: e.activation(out=pt[0:64, :, 0:64], in_=stv[0:64, :, 0:64],
                                                         func=AF.Exp, scale=0.125), rd=[stb], wr=[pt], c=0.4)
                        op("act", lambda e: e.activation(out=pt[64:128, :, :], in_=stv[64:128, :, :],
                                                         func=AF.Exp, scale=0.125), rd=[stb], wr=[pt], c=0.6)
                    elif mask_block == "cur":
                        op("act", lambda e: e.activation(out=pt[0:64, :, :], in_=stv[0:64, :, :],
                                                         func=AF.Exp, scale=0.125), rd=[stb], wr=[pt], c=0.6)
                        op("act", lambda e: e.activation(out=pt[64:128, :, 64:128], in_=stv[64:128, :, 64:128],
                                                         func=AF.Exp, scale=0.125), rd=[stb], wr=[pt], c=0.4)
                    else:
                        so = stv[:, :, qsl] if nq == 128 else stb[:, 0:4 * nq].rearrange("p (a b) -> p a b", a=4)
                        op("act", lambda e: e.activation(out=pt[:, :, qsl], in_=so, func=AF.Exp,
                                                         scale=0.125), rd=[stb], wr=[pt])
                    if use_msk:
                        op("pool", lambda e: e.tensor_tensor(
                            out=pt[:], in0=pt[:], in1=MSK[:].unsqueeze(1).to_broadcast([128, 4, 128]),
                            op=ALU.mult), rd=[pt, MSK], wr=[pt])

                def pv():
                    for j in range(4):
                        h = 4 * g + j
                        pv_mm("A", h // 4, (h % 4) * 65, pt[:, j, :], VA[:, slot, g, :], [pt, var_],
                              last=(j == 3))
                return {"qk": qk, "exp": ex, "pv": pv, "nohoist": pre is not None}

            steps = []
            if sample:
                for s in range(4):
                    def pre(s=s):
                        dma("pool", CAKb[:], cak[s], wr=[CAKb])
                        dma("pool", VA[:, 1, :, 0:64], cav[s].rearrange("k (h d) -> k h d", h=2),
                            wr=[ring(VA, 1)])
                        S.acquire("TR0")
                        bb_ = transposes_bf(CAKb, lambda i: CAKb[:], 1, TR[0])
                        op("dve", lambda e: e.tensor_copy(out=KAT[:, 1, :], in_=bb_[:, 0:128]), rd=[TR[0]],
                           wr=[ring(KAT, 1)])
                        S.release("TR0")
                    for g in halves:
                        steps.append(a_step(1, g, slice(32 * s, 32 * s + 32), 32, None, False, False, s % 2,
                                            pre=pre if g == 0 else None))
                for g in halves:
                    steps.append(a_step(0, g, allq, 128, None, True, True, 0))
            else:
                if m > 0:
                    for g in halves:
                        steps.append(a_step((m - 1) % 2, g, allq, 128, "prev", False, False, 0))
                for g in halves:
                    steps.append(a_step(m % 2, g, allq, 128, "cur", False, True, 1))
            yield from run_steps(steps)

            def pv_finish(first_col, sink):
                for bi in halves:
                    bank = PV[bi]
                    PVT = PVT2[bi]
                    denv = ring(DEN, bi)
                    yv = View(Y.t, [Y.bufs[1 + (first_col + bi * 256) // 256]])
                    pvv = bank[:, 0:260].rearrange("p (h d) -> p h d", h=4)
                    dn = DEN[:, bi * 4:bi * 4 + 4]
                    if sink:
                        op("dve", lambda e: e.tensor_tensor(out=dn, in0=pvv[:, :, 64], in1=EXS[:, bi * 4:bi * 4 + 4],
                                                            op=ALU.add), rd=[bank, EXS], wr=[denv])
                        op("dve", lambda e: e.reciprocal(out=dn, in_=dn), rd=[denv], wr=[denv])
                    else:
                        op("dve", lambda e: e.reciprocal(out=dn, in_=pvv[:, :, 64]), rd=[bank], wr=[denv])
                    op("dve", lambda e: e.tensor_tensor(out=PVT[:], in0=pvv[:, :, 0:64],
                                                        in1=dn.unsqueeze(2).to_broadcast([128, 4, 64]),
                                                        op=ALU.mult), rd=[bank, denv], wr=[PVT])
                    c0 = first_col + bi * 256
                    op("pool", lambda e: e.tensor_tensor(
                        out=Y[:, c0:c0 + 256], in0=PVT[:].rearrange("p h d -> p (h d)"), in1=GS[:, c0:c0 + 256],
                        op=ALU.mult), rd=[PVT, GS], wr=[yv])

            pv_finish(0, True)
            yield

            def b_step(slot, hg, qsl, nq, bias_fn, pre=None):
                kbr_ = ring(KBT, slot)
                vbr_ = ring(VB, slot)
                box = {}

                def qk():
                    if pre is not None:
                        pre()
                    stb = pick_st(hg)
                    box["stb"] = stb
                    stv = stb[:, :].rearrange("p (a b) -> p a b", a=4)
                    so = stv[:, :, qsl] if nq == 128 else stb[:, 0:4 * nq].rearrange("p (a b) -> p a b", a=4)
                    for jp in range(2):
                        c = 2 * hg + jp
                        op("pe", lambda e: e.matmul(so[:, 2 * jp:2 * jp + 2, :], lhsT=KBT[:, c, slot, :],
                                                    rhs=QBT[:, c, :, qsl],
                                                    start=(jp == 0), stop=False, skip_group_check=True),
                           rd=[kbr_, QBT], wr=[stb], defer=True, c=0.2 if nq == 128 else 0.1)
                    brhs, bview = bias_fn(hg)
                    op("pe", lambda e: e.matmul(so, lhsT=JB[:], rhs=brhs, start=False, stop=True,
                                                skip_group_check=True),
                       rd=[JB, bview], wr=[stb], c=0.34 if nq == 128 else 0.12)

                def ex():
                    stb = box["stb"]
                    stv = stb[:, :].rearrange("p (a b) -> p a b", a=4)
                    pt = pick_ptb(hg)
                    box["pt"] = pt
                    if nq < 128:
                        op("pool", lambda e: e.memset(pt[:], 0.0), wr=[pt])
                    so = stv[:, :, qsl] if nq == 128 else stb[:, 0:4 * nq].rearrange("p (a b) -> p a b", a=4)
                    op("act", lambda e: e.activation(out=pt[:, :, qsl], in_=so, func=AF.Exp,
                                                     scale=0.125), rd=[stb], wr=[pt])

                def pv():
                    pt = box["pt"]
                    for j in range(4):
                        h = 4 * hg + j
                        pv_mm("B", hg, j * 65, pt[:, j, :], VB[:, slot, h, :], [pt, vbr_], last=(j == 3))
                return {"qk": qk, "exp": ex, "pv": pv, "nohoist": pre is not None}

            steps = []
            if sample:
                for s in range(4):
                    def pre(s=s):
                        dma("pool", CBKb[:], cbk[s].rearrange("(t k) f -> k t f", k=128), wr=[CBKb])
                        for t in range(4):
                            dma("pool", VB[:, t, :, 0:64],
                                cbv[s, t * 128:(t + 1) * 128, :].rearrange("k (h d) -> k h d", h=8),
                                wr=[ring(VB, t)])
                        for t in range(4):
                            S.acquire(f"TR{t % 2}")
                            bb_ = transposes_bf(CBKb, lambda i, t=t: CBKb[:, t, i * 128:(i + 1) * 128], 4, TR[t % 2])
                            op("dve", lambda e, t=t, bb_=bb_: e.tensor_copy(
                                out=KBT[:, :, t, :], in_=bb_[:, 0:512].rearrange("p (a b) -> p a b", a=4)),
                                rd=[TR[t % 2]], wr=[ring(KBT, t)])
                            S.release(f"TR{t % 2}")
                    qsl = slice(32 * s, 32 * s + 32)
                    for t in range(4):
                        bi_ = 2 if t < 3 else 1
                        for hg in halves:
                            steps.append(b_step(t, hg, qsl, 32,
                                                lambda hg_, bi_=bi_: (BT[:, 4 * hg_:4 * hg_ + 4, bi_, 0:32], BT),
                                                pre=pre if (t == 0 and hg == 0) else None))
                for hg in halves:
                    steps.append(b_step(4, hg, allq, 128, lambda hg_: (BSN[:, 4 * hg_:4 * hg_ + 4, :], BSN)))
            else:
                dts = [d for d in (4, 3, 2, 1, 0) if m - d >= 0]
                for d in dts:
                    bi_ = {0: 0, 1: 1, 2: 2, 3: 2, 4: 3}[d]
                    for hg in halves:
                        steps.append(b_step((m - d) % 5, hg, allq, 128,
                                            lambda hg_, bi_=bi_: (BT[:, 4 * hg_:4 * hg_ + 4, bi_, :], BT)))
            yield from run_steps(steps)
            pv_finish(512, False)
            yield


        def gen_L0(tno, kind, seq, m):
            sample = kind == "s"
            ti = tno % 2
            Xt = X[tno % 2]
            slotA = 0 if sample else m % 2
            slotB = 4 if sample else m % 5
            out_a = (not sample) and m == NT - 1
            out_b = sample or m >= NT - 4

            prenorm(Xt, 0)
            hT = hTL[0]
            yield

            def inproj(cg, ncols):
                bank = IP[cg % 2]
                S.acquire(f"IP{cg % 2}")
                for k in range(8):
                    op("pe", lambda e, k=k: e.matmul(bank[:, 0:ncols], lhsT=hT[:, k, :],
                                                     rhs=Win[:, k, cg * 512:cg * 512 + ncols],
                                                     start=(k == 0), stop=(k == 7)),
                       rd=[hT, Win], wr=[bank], defer=(k < 7), c=0.34 if ncols == 512 else 0.2)
                return bank

            bank = inproj(0, 512)
            rope(bank, 0, 8, ti, QA[:], QA)
            S.release("IP0")
            yield
            bank = inproj(1, 512)
            op("act", lambda e: e.copy(out=QB[:], in_=bank[:, :]), rd=[bank], wr=[QB])
            S.release("IP1")
            S.acquire("TR0")
            bb = transposes_bf(QA, lambda i: QA[:, i * 128:(i + 1) * 128], 4, TR[0])
            bbv = bb[:, 0:512].rearrange("p (a b) -> p a b", a=4)
            op("dve", lambda e: e.tensor_copy(out=QAT[0:64, 0, :, :], in_=bbv[0:64]), rd=[TR[0]], wr=[QAT])
            op("act", lambda e: e.copy(out=QAT[64:128, 1, :, :], in_=bbv[64:128]), rd=[TR[0]], wr=[QAT])
            S.release("TR0")
            yield
            bank = inproj(2, 512)
            op("act", lambda e: e.copy(out=KB[:], in_=bank[:, :]), rd=[bank], wr=[KB])
            if out_b:
                op("dve", lambda e: e.tensor_copy(out=KBf[:], in_=bank[:, :]), rd=[bank], wr=[KBf])
            S.release("IP0")
            S.acquire("TR1")
            bb = transposes_bf(QB, lambda i: QB[:, i * 128:(i + 1) * 128], 4, TR[1])
            bbv = bb[:, 0:512].rearrange("p (a b) -> p a b", a=4)
            op("dve", lambda e: e.tensor_copy(out=QBT[0:64, :, 0, :], in_=bbv[0:64]), rd=[TR[1]], wr=[QBT])
            op("act", lambda e: e.copy(out=QBT[64:128, :, 1, :], in_=bbv[64:128]), rd=[TR[1]], wr=[QBT])
            S.release("TR1")
            yield
            bank = inproj(3, 512)
            vbr = ring(VB, slotB)
            op("dve", lambda e: e.tensor_copy(out=VB[:, slotB, :, 0:64],
                                              in_=bank[:, :].rearrange("p (h d) -> p h d", h=8)),
               rd=[bank], wr=[vbr])
            if out_b:
                op("act", lambda e: e.copy(out=VBf[:], in_=bank[:, :]), rd=[bank], wr=[VBf])
                if sample:
                    for s in range(4):
                        r0 = 32 * s
                        dma("sp", o_bks[s, 480:512, :], KBf[r0:r0 + 32, :], rd=[KBf], store=True)
                        dma("sp", o_bvs[s, 480:512, :], VBf[r0:r0 + 32, :], rd=[VBf], store=True)
                else:
                    r0 = (m - (NT - 4)) * 128
                    dma("sp", o_bkp[seq, r0:r0 + 128, :], KBf[:], rd=[KBf], store=True)
                    dma("sp", o_bvp[seq, r0:r0 + 128, :], VBf[:], rd=[VBf], store=True)
            S.release("IP1")
            S.acquire("TR0")
            bb = transposes_bf(KB, lambda i: KB[:, i * 128:(i + 1) * 128], 4, TR[0])
            kbr = ring(KBT, slotB)
            op("dve", lambda e: e.tensor_copy(out=KBT[:, :, slotB, :],
                                              in_=bb[:, 0:512].rearrange("p (a b) -> p a b", a=4)),
               rd=[TR[0]], wr=[kbr])
            S.release("TR0")
            yield
            bank = inproj(6, 256)
            rope(bank, 0, 2, ti, KAf[:], KAf)
            op("pool", lambda e: e.tensor_copy(out=KAb[:], in_=KAf[:]), rd=[KAf], wr=[KAb])
            var = ring(VA, slotA)
            op("dve", lambda e: e.tensor_copy(out=VA[:, slotA, :, 0:64],
                                              in_=bank[:, 128:256].rearrange("p (h d) -> p h d", h=2)),
               rd=[bank], wr=[var])
            if out_a or sample:
                op("act", lambda e: e.copy(out=VAf[:], in_=bank[:, 128:256]), rd=[bank], wr=[VAf])
            S.release("IP0")
            yield
            for cg in (4, 5):
                bank = inproj(cg, 512)
                silu_from(bank[:, :], GS[:, (cg - 4) * 512:(cg - 3) * 512], GS, bank, GTMP[:], GTMP)
                S.release(f"IP{cg % 2}")
                if cg == 4:
                    S.acquire("TR1")
                    bb = transposes_bf(KAb, lambda i: KAb[:], 1, TR[1])
                    kar = ring(KAT, slotA)
                    op("dve", lambda e: e.tensor_copy(out=KAT[:, slotA, :], in_=bb[:, 0:128]), rd=[TR[1]],
                       wr=[kar])
                    S.release("TR1")
                yield

            if sample:
                for s in range(4):
                    r0 = 32 * s
                    dma("sp", o_aks[s, 96:128, :], KAf[r0:r0 + 32, :], rd=[KAf], store=True)
                    dma("sp", o_avs[s, 96:128, :], VAf[r0:r0 + 32, :], rd=[VAf], store=True)
                    dma("sp", o_aks[s, 0:96, :], cak[s, 32:128, :], store=True)
                    dma("sp", o_avs[s, 0:96, :], cav[s, 32:128, :], store=True)
                    dma("sp", o_bks[s, 0:480, :], cbk[s, 32:512, :], store=True)
                    dma("sp", o_bvs[s, 0:480, :], cbv[s, 32:512, :], store=True)
            else:
                if out_a:
                    dma("sp", o_akp[seq], KAf[:], rd=[KAf], store=True)
                    dma("sp", o_avp[seq], VAf[:], rd=[VAf], store=True)

            split = not sample
            if tno == 1:
                for pt_ in PTA:
                    op("pool", lambda e, pt_=pt_: e.memset(pt_[:], 0.0), wr=[pt_])
            if split:
                S.mark(f"attn{tno}")
                yield from gen_attn(tno, kind, seq, m, (0,))
                S.wait_mark(f"Ydone{tno}")
            else:
                yield from gen_attn(tno, kind, seq, m, (0, 1))
            S.acquire("TR0")
            bb0 = transposes_bf(Y, lambda i: Y[:, i * 128:(i + 1) * 128], 4, TR[0])
            op("dve", lambda e: e.tensor_copy(out=YT[:, 0:4, :], in_=bb0[:, 0:512].rearrange("p (a b) -> p a b", a=4)),
               rd=[TR[0]], wr=[YT])
            S.release("TR0")

            S.acquire("TR1")
            bb1 = transposes_bf(Y, lambda i: Y[:, 512 + i * 128:512 + (i + 1) * 128], 4, TR[1])
            op("act", lambda e: e.copy(out=YT[:, 4:8, :], in_=bb1[:, 0:512].rearrange("p (a b) -> p a b", a=4)),
               rd=[TR[1]], wr=[YT])
            S.release("TR1")
            S.acquire("IP0")
            S.acquire("IP1")
            for n in range(2):
                for k in range(8):
                    op("pe", lambda e, k=k, n=n: e.matmul(IP[n][:, :], lhsT=YT[:, k, :],
                                                          rhs=Wout0[:, k, n * 512:(n + 1) * 512],
                                                          start=(k == 0), stop=(k == 7)),
                       rd=[YT, Wout0], wr=[IP[n]], defer=(k < 7), c=0.34)
            postnorm(Xt, 0)
            S.release("IP0")
            S.release("IP1")
            yield

        def gen_L1_pre(tno, kind, seq, m):
            sample = kind == "s"
            Xt = X[tno % 2]
            prenorm(Xt, 1)
            if sample:
                S.acquire("TR0")
                for s in range(4):
                    for half in range(2):
                        dma("sp", S4[0:4, :], st4[s, :, half * 512:(half + 1) * 512], wr=[S4])
                        for c4 in range(4):
                            c = half * 4 + c4
                            op("pe", lambda e, c=c, c4=c4: e.transpose(out=TR[0][:, c * 4:c * 4 + 4],
                                                                       in_=S4[0:4, c4 * 128:(c4 + 1) * 128],
                                                                       identity=IDF[0:4, 0:4]),
                               rd=[S4, IDF], wr=[TR[0]], defer=(c4 < 3))
                    op("dve", lambda e, s=s: e.tensor_copy(
                        out=INIT[:, :, s, :], in_=TR[0][:, 0:32].rearrange("p (c f) -> p c f", c=8)),
                        rd=[TR[0]], wr=[INIT])
                S.release("TR0")
            elif m == 0:
                op("pool", lambda e: e.memset(HIST[:], 0.0), wr=[HIST])
                op("pool", lambda e: e.memset(HST[:], 0.0), wr=[HST])
            yield

        def gen_L1_pieces(tno, kind, seq, m, qs, P):
            sample = kind == "s"
            hT = hTL[1]
            need_state = sample or m == NT - 1
            pXBs, pXBp, pXB, pXC, pXCb, pLAB, pLC, pHS, pZS = P.XBs, P.XBp, P.XB, P.XC, P.XCb, P.LAB, P.LC, P.HS, P.ZS
            for q in qs:
                c0 = 2 * q
                hist, hst, st4v, y1t = ring(HIST, q), ring(HST, q), ring(ST4, q), ring(Y1T, q)
                bank = IP[q % 2]
                S.acquire(f"IP{q % 2}")
                ocs = (c0, c0 + 1, 8 + c0, 9 + c0)
                for i, oc in enumerate(ocs):
                    for k in range(8):
                        op("pe", lambda e, k=k, oc=oc, i=i: e.matmul(
                            bank[:, i * 128:(i + 1) * 128], lhsT=Winc[:, k, oc * 128:(oc + 1) * 128],
                            rhs=hT[:, k, :], start=(k == 0 and i == 0), stop=(k == 7),
                            skip_group_check=True),
                            rd=[Winc, hT], wr=[bank], defer=not (k == 7 and i == 3))
                if sample:
                    op("pool", lambda e: e.tensor_copy(out=pXBs[:, :, :, 0:3], in_=INIT[:, c0:c0 + 2, :, 1:4]),
                       rd=[INIT], wr=[pXB])
                    op("dve", lambda e: e.tensor_copy(
                        out=pXBs[:, :, :, 3:35], in_=bank[:, 0:256].rearrange("p (a s t) -> p a s t", a=2, s=4)),
                        rd=[bank], wr=[pXB])
                else:
                    op("pool", lambda e: e.tensor_copy(out=pXBp[:, :, 0:3], in_=HIST[:, c0:c0 + 2, :]),
                       rd=[hist], wr=[pXB])
                    op("dve", lambda e: e.tensor_copy(
                        out=pXBp[:, :, 3:131], in_=bank[:, 0:256].rearrange("p (a t) -> p a t", a=2)),
                        rd=[bank], wr=[pXB])
                silu_from(bank[:, 256:512].rearrange("p (a t) -> p a t", a=2), pZS[:], pZS, bank, pLC[:], pLC)
                S.release(f"IP{q % 2}")
                yield
                for j in range(2):
                    c = c0 + j
                    if sample:
                        tap_ap = lambda tap, j=j: pXBs[:, j, :, tap:tap + 32]
                        xc_ap = pXC[:, j, :].rearrange("p (s t) -> p s t", s=4)
                    else:
                        tap_ap = lambda tap, j=j: pXBp[:, j, tap:tap + 128]
                        xc_ap = pXC[:, j, :]
                    op("dve", lambda e, c=c, xc_ap=xc_ap, tap_ap=tap_ap: e.tensor_scalar(
                        out=xc_ap, in0=tap_ap(3), scalar1=VTv[:, 3, c:c + 1], scalar2=VTv[:, 4, c:c + 1],
                        op0=ALU.mult, op1=ALU.add), rd=[pXB, VT], wr=[pXC], c=0.32)
                    for tap in (2, 1, 0):
                        op("dve", lambda e, c=c, tap=tap, xc_ap=xc_ap, tap_ap=tap_ap: e.scalar_tensor_tensor(
                            out=xc_ap, in0=tap_ap(tap), scalar=VTv[:, tap, c:c + 1], in1=xc_ap,
                            op0=ALU.mult, op1=ALU.add), rd=[pXB, VT, pXC], wr=[pXC], c=0.44)
                op("act", lambda e: e.copy(out=pXCb[:], in_=pXC[:]), rd=[pXC], wr=[pXCb], c=0.4)
                if not sample:
                    op("pool", lambda e: e.tensor_copy(out=HIST[:, c0:c0 + 2, :], in_=pXBp[:, :, 128:131]),
                       rd=[pXB], wr=[hist], c=0.25)
                if need_state:
                    if sample:
                        op("pool", lambda e: e.tensor_copy(out=ST4[:, c0:c0 + 2, :, 1:4], in_=pXBs[:, :, :, 32:35]),
                           rd=[pXB], wr=[st4v])
                    else:
                        op("pool", lambda e: e.tensor_copy(out=ST4[:, c0:c0 + 2, 0, 1:4], in_=pXBp[:, :, 128:131]),
                           rd=[pXB], wr=[st4v])
                yield
                gb = TR[q % 2]
                S.acquire(f"TR{q % 2}")
                for i in range(4):
                    bd = BDa if i < 2 else BDx
                    j = i % 2
                    op("pe", lambda e, i=i, j=j, bd=bd: e.matmul(
                        gb[:, i * 128:(i + 1) * 128], lhsT=bd[:, c0 + j, :], rhs=pXCb[:, j, :],
                        start=(i == 0), stop=True, skip_group_check=True),
                        rd=[bd, pXCb], wr=[gb], defer=(i < 3))
                bias4 = VTv[:, 5:7, c0:c0 + 2].unsqueeze(3).to_broadcast([128, 2, 2, 128])
                op("dve", lambda e: e.tensor_tensor(
                    out=pLAB[:].rearrange("p (v c) t -> p v c t", v=2),
                    in0=gb[:, :].rearrange("p (v c t) -> p v c t", v=2, c=2), in1=bias4, op=ALU.add),
                    rd=[gb, VT], wr=[pLAB], c=0.7)
                S.release(f"TR{q % 2}")
                sigmoid_inplace(pLAB[:], pLAB)
                yield
                for j in range(2):
                    op("act", lambda e, j=j: e.activation(out=pLAB[:, j, :], in_=pLAB[:, j, :], func=AF.Exp,
                                                          scale=CLOG[:, c0 + j:c0 + j + 1]),
                       rd=[pLAB, CLOG], wr=[pLAB], c=0.4)
                op("pool", lambda e: e.tensor_tensor(out=pLAB[:, 2:4, :], in0=pLAB[:, 2:4, :], in1=pXC[:],
                                                     op=ALU.mult), rd=[pLAB, pXC], wr=[pLAB], c=0.7)
                op("dve", lambda e: e.scalar_tensor_tensor(
                    out=pLC[:].rearrange("p c t -> p (c t)"), in0=pLAB[:, 0:2, :].rearrange("p c t -> p (c t)"),
                    scalar=-1.0, in1=pLAB[:, 0:2, :].rearrange("p c t -> p (c t)"), op0=ALU.mult, op1=ALU.mult),
                    rd=[pLAB], wr=[pLC], c=0.45)
                op("act", lambda e: e.activation(out=pLC[:], in_=pLC[:], func=AF.Ln, bias=1.0, scale=1.0),
                   rd=[pLC], wr=[pLC], c=0.42)
                op("act", lambda e: e.activation(out=pLC[:], in_=pLC[:], func=AF.Exp, scale=0.5),
                   rd=[pLC], wr=[pLC], c=0.42)
                op("pool", lambda e: e.tensor_tensor(out=pLAB[:, 2:4, :], in0=pLAB[:, 2:4, :], in1=pLC[:],
                                                     op=ALU.mult), rd=[pLAB, pLC], wr=[pLAB], c=0.7)
                yield
                if sample:
                    for j in range(2):
                        for s in range(4):
                            sl = slice(32 * s, 32 * s + 32)
                            op("dve", lambda e, j=j, s=s, sl=sl: e.tensor_tensor_scan(
                                out=pHS[:, j, sl], data0=pLAB[:, j, sl], data1=pLAB[:, 2 + j, sl],
                                initial=INIT[:, c0 + j, s, 0:1], op0=ALU.mult, op1=ALU.add),
                                rd=[pLAB, INIT], wr=[pHS], defer=not (j == 1 and s == 3), c=0.25)
                    op("pool", lambda e: e.tensor_copy(
                        out=ST4[:, c0:c0 + 2, :, 0],
                        in_=pHS[:].rearrange("p c (s t) -> p c s t", s=4)[:, :, :, 31]), rd=[pHS], wr=[st4v])
                else:
                    for j in range(2):
                        op("dve", lambda e, j=j: e.tensor_tensor_scan(
                            out=pHS[:, j, :], data0=pLAB[:, j, :], data1=pLAB[:, 2 + j, :],
                            initial=HST[:, c0 + j, 0:1], op0=ALU.mult, op1=ALU.add),
                            rd=[pLAB, hst], wr=[pHS], defer=(j == 0), c=0.44)
                    op("dve", lambda e: e.tensor_copy(out=HST[:, c0:c0 + 2, 0], in_=pHS[:, :, 127]),
                       rd=[pHS], wr=[hst], c=0.2)
                    if need_state:
                        op("pool", lambda e: e.tensor_copy(out=ST4[:, c0:c0 + 2, 0, 0], in_=pHS[:, :, 127]),
                           rd=[pHS], wr=[st4v])
                op("pool", lambda e: e.tensor_tensor(out=Y1T[:, c0:c0 + 2, :], in0=pHS[:], in1=pZS[:], op=ALU.mult),
                   rd=[pHS, pZS], wr=[y1t], c=0.7)
                yield

        def gen_L1_post(tno, kind, seq, m):
            sample = kind == "s"
            Xt = X[tno % 2]

            def state_out(s, dst_h, dst_c):
                S.acquire("TR1")
                for half in range(2):
                    for c4 in range(4):
                        c = half * 4 + c4
                        op("pe", lambda e, c=c, c4=c4: e.transpose(out=TR[1][0:4, c4 * 128:(c4 + 1) * 128],
                                                                   in_=ST4[:, c, s, :], identity=IDF[:]),
                           rd=[ST4, IDF], wr=[TR[1]], defer=(c4 < 3))
                    op("dve", lambda e: e.tensor_copy(out=O4[0:4, :], in_=TR[1][0:4, :]), rd=[TR[1]], wr=[O4])
                    dma("sp", dst_h[:, half * 512:(half + 1) * 512], O4[0:1, :], rd=[O4], store=True)
                    dma("sp", dst_c[:, half * 512:(half + 1) * 512], O4[1:4, :], rd=[O4], store=True)
                S.release("TR1")

            if sample:
                for s in range(4):
                    state_out(s, o_chs[s:s + 1, :], o_ccs[s])
            elif m == NT - 1:
                state_out(0, o_chp[seq:seq + 1, :], o_ccp[seq])

            S.acquire("IP0")
            S.acquire("IP1")
            for n in range(2):
                for k in range(8):
                    op("pe", lambda e, k=k, n=n: e.matmul(IP[n][:, :], lhsT=Y1T[:, k, :],
                                                          rhs=Wout1[:, k, n * 512:(n + 1) * 512],
                                                          start=(k == 0), stop=(k == 7)),
                       rd=[Y1T, Wout1], wr=[IP[n]], defer=(k < 7), c=0.34)
            postnorm(Xt, 1)
            S.release("IP0")
            S.release("IP1")
            if sample:
                dma("sp", o_ys, Xt[:], rd=[Xt], store=True)
            else:
                dma("sp", o_yp[seq, m * 128:(m + 1) * 128, :], Xt[:], rd=[Xt], store=True)
            yield

        tiles = [("s", 0, 0)] + [("p", seq, m) for seq in range(2) for m in range(NT)]
        if NTILES_DEBUG is not None:
            tiles = tiles[:NTILES_DEBUG]
        nt_ = len(tiles)
        for bi in range(2):
            call = ("matmul", (PV[bi][:, 264:512],),
                    dict(lhsT=IDB[:], rhs=Wout0[:, 0, 0:248], start=False, stop=True, skip_group_check=True))
            S.dummy_items.append(("op", "pe", _replay(call), [IDB, Wout0], [PV[bi]], False, KA_COST, None))

        def stream_A():
            for i, t in enumerate(tiles):
                S.wait_mark(f"L0done{i}")
                yield from gen_L1_pre(i, *t)
                S.mark(f"pre{i}")
                yield from gen_L1_pieces(i, *t, (0, 1), PS[0])
                S.wait_mark(f"Bdone{i}")
                yield from gen_L1_post(i, *t)
                S.mark(f"L1done{i}")

        def stream_B():
            for i, t in enumerate(tiles):
                S.wait_mark(f"pre{i}")
                yield from gen_L1_pieces(i, *t, (2, 3), PS[1])
                S.mark(f"Bdone{i}")

        def stream_C():
            for i, t in enumerate(tiles):
                if i >= 2:
                    S.wait_mark(f"L1done{i - 2}")
                load_x(i)
                yield from gen_L0(i, *t)
                S.mark(f"L0done{i}")

        def stream_D():
            for i, t in enumerate(tiles):
                if t[0] == "s":
                    continue
                S.wait_mark(f"attn{i}")
                yield from gen_attn(i, *t, (1,))
                S.mark(f"Ydone{i}")

        if PIPELINE:
            lists = [S.record(stream_A()), S.record(stream_B()), S.record(stream_C()), S.record(stream_D()),
                     S.record(gen_weights())]
            if DAG:
                snap = S.snapshot()
                S.dry, S.trace = True, []
                S.run_streams(lists)
                order, S.trace, S.dry = S.trace, None, False
                t_streams = max(S.free_t.values())
                S.restore(snap)
                if DAG_ORIG == 2:
                    for it_ in order:
                        S.emit(it_)
                else:
                    S.run_dag(order, DAG_DELTA)
                print("stream-order makespan %.1f -> dag makespan %.1f, keepalive matmuls %d" % (
                    t_streams, max(S.free_t.values()), S.n_dummy))
            else:
                S.run_streams(lists)
        else:
            for i, t in enumerate(tiles):
                if i == 0:
                    S.run_streams([S.record(gen_weights())])
                load_x(i)
                if t[0] == "s":
                    S.run_streams([S.record(gen_L0(i, *t))])
                else:
                    def d_(i=i, t=t):
                        S.wait_mark(f"attn{i}")
                        yield from gen_attn(i, *t, (1,))
                        S.mark(f"Ydone{i}")
                    S.run_streams([S.record(gen_L0(i, *t)), S.record(d_())])
                S.run_streams([S.record(gen_L1_pre(i, *t))])
                S.run_streams([S.record(gen_L1_pieces(i, *t, (0, 1, 2, 3), PS[0]))])
                S.run_streams([S.record(gen_L1_post(i, *t))])
        S.finish()
        print("simulated makespan (us):", round(max(S.free_t.values()), 1), {k: round(v) for k, v in S.busy.items()})
    return nc


_CACHE = {}


def _rope_tables():
    half = 32
    inv = 10000.0 ** (-np.arange(half, dtype=np.float64) / half)
    pos = np.zeros((128, 17), np.float64)
    p = np.arange(128)
    pos[:, 0] = 1024 + (p % 32)
    for i in range(16):
        pos[:, 1 + i] = i * 128 + p
    ang = pos[:, :, None] * inv[None, None, :]
    return np.cos(ang).astype(np.float32), np.sin(ang).astype(np.float32)


def kernel(x_prompt, x_sample, cache_a_k, cache_a_v, cache_b_k, cache_b_v, state_c_h, state_c_conv,
           ln_pre, ln_post, w_in_ab, sinks_a, relpos_b, w_out_ab, w_in_c, conv_c_w, conv_c_b,
           gate_c_wa, gate_c_ba, gate_c_wx, gate_c_bx, lambda_c, w_out_c):
    f = lambda a: np.ascontiguousarray(np.asarray(a, dtype=np.float32))
    x_prompt, x_sample = f(x_prompt), f(x_sample)
    if "nc" not in _CACHE:
        _CACHE["nc"] = build_program()
    nc = _CACHE["nc"]

    w = f(w_in_ab)[0]
    qa_cols = np.concatenate([np.arange(h * 64, (h + 1) * 64) for h in (0, 4, 1, 5, 2, 6, 3, 7)])
    cols = np.concatenate([qa_cols, np.arange(1280, 1792), np.arange(1792, 2304), np.arange(2304, 2816),
                           np.arange(768, 1280), np.arange(2816, 3328), np.arange(512, 640),
                           np.arange(640, 768)])
    w_in = np.ascontiguousarray(w[:, cols])
    tab = np.pad(f(relpos_b)[0], ((0, 0), (0, TABL - 257)), mode="edge")
    vec_rows = np.concatenate([f(conv_c_w)[0], f(conv_c_b), f(gate_c_ba), f(gate_c_bx), f(lambda_c),
                               f(ln_pre)], axis=0)
    vecs = np.ascontiguousarray(vec_rows.reshape(80, 128))
    rc, rs = _rope_tables()
    sid = np.arange(128) // 32
    cmask = (sid[:, None] == sid[None, :]).astype(np.float32)
    shared = {
        "vecs": vecs, "lnpost": f(ln_post), "w_in": w_in, "w_out0": f(w_out_ab)[0], "w_inc": f(w_in_c)[0],
        "w_out1": f(w_out_c)[0], "sinks": f(sinks_a)[0], "tab": np.ascontiguousarray(tab),
        "gwa": f(gate_c_wa)[0], "gwx": f(gate_c_wx)[0], "ropec": rc, "ropes": rs, "cmask": cmask,
    }
    cak, cav = f(cache_a_k)[0].reshape(32, 128, 128), f(cache_a_v)[0].reshape(32, 128, 128)
    cbk, cbv = f(cache_b_k)[0].reshape(32, 512, 512), f(cache_b_v)[0].reshape(32, 512, 512)
    st4 = np.concatenate([f(state_c_h)[0][:, None, :], f(state_c_conv)[0]], axis=1)
    in_maps = []
    for c in range(NCORES):
        d = dict(shared)
        d["xp"] = np.ascontiguousarray(x_prompt[2 * c:2 * c + 2])
        d["xs"] = np.ascontiguousarray(x_sample[4 * c:4 * c + 4].reshape(128, D))
        d["cak"] = np.ascontiguousarray(cak[4 * c:4 * c + 4])
        d["cav"] = np.ascontiguousarray(cav[4 * c:4 * c + 4])
        d["cbk"] = np.ascontiguousarray(cbk[4 * c:4 * c + 4])
        d["cbv"] = np.ascontiguousarray(cbv[4 * c:4 * c + 4])
        d["st4"] = np.ascontiguousarray(st4[4 * c:4 * c + 4])
        in_maps.append(d)
    res = run_bass_kernel_spmd(nc, in_maps, core_ids=list(range(NCORES)))
    R = res.results
    cat = lambda k: np.concatenate([r[k] for r in R], axis=0)
    y_p = cat("o_yp")
    y_s = cat("o_ys").reshape(32, 32, D)
    akp = cat("o_akp").reshape(1, 16, 128, 2, 64)
    avp = cat("o_avp").reshape(1, 16, 128, 2, 64)
    bkp = cat("o_bkp").reshape(1, 16, 512, 8, 64)
    bvp = cat("o_bvp").reshape(1, 16, 512, 8, 64)
    chp = cat("o_chp").reshape(1, 16, D)
    ccp = cat("o_ccp").reshape(1, 16, 3, D)
    aks = cat("o_aks").reshape(1, 32, 128, 2, 64)
    avs = cat("o_avs").reshape(1, 32, 128, 2, 64)
    bks = cat("o_bks").reshape(1, 32, 512, 8, 64)
    bvs = cat("o_bvs").reshape(1, 32, 512, 8, 64)
    chs = cat("o_chs").reshape(1, 32, D)
    ccs = cat("o_ccs").reshape(1, 32, 3, D)
    return (y_p, y_s, akp, avp, bkp, bvp, chp, ccp, aks, avs, bks, bvs, chs, ccs)
```

```python
import numpy as np
from contextlib import ExitStack
import concourse.bass as bass
import concourse.mybir as mybir
from concourse.bass_utils import run_bass_kernel_spmd

F32 = mybir.dt.float32
BF16 = mybir.dt.bfloat16
AF = mybir.ActivationFunctionType
ALU = mybir.AluOpType

NCORES = 8
D = 1024
SEQ = 2048
NT = SEQ // 128
TABL = 800
EPS = 1e-6
NEG = -30000.0
N_DMA_SEMS = 24
NTILES_DEBUG = None
PIPELINE = True
DAG = True
DAG_DELTA = 0.3
KEEPALIVE = True
KA_MIN_GAP = 0.6
KA_FILL = 0.5
KA_COST = 0.11
DAG_ORIG = False


class Buf:
    __slots__ = ("name", "w", "r", "excl")

    ALL = []

    def __init__(self, name, excl=False):
        Buf.ALL.append(self)
        self.name = name
        self.excl = excl
        self.w = None
        self.r = []


class View:
    def __init__(self, t, bufs):
        self.t = t
        self.bufs = bufs

    def __getitem__(self, k):
        return self.t[k]


class Eng:
    def __init__(self, name, h, sem):
        self.name, self.h, self.sem = name, h, sem
        self.count = 0
        self.waited = {}
        self.pend_r, self.pend_w = [], []


class _Proxy:
    def __init__(self):
        self.call = None

    def __getattr__(self, name):
        def f(*a, **k):
            assert self.call is None
            self.call = (name, a, k)
            return None
        return f


def _replay(call):
    name, a, k = call
    return lambda h: getattr(h, name)(*a, **k)


PSEUDO = ("acq", "rel", "mark", "wait")
DEF_COST = {"pe": 0.11, "act": 0.5, "dve": 0.45, "pool": 0.6, "sp": 0.1}
HOP = 0.1
PRIO = [0.0, 0.0, 0.0, 0.0, 0.0]
COST_SCALE = {"pe": 1.0, "act": 1.0, "dve": 1.0, "pool": 1.0, "sp": 1.0}


class Sched:
    def __init__(self, nc, es):
        self.nc = nc
        self.es = es
        self.nsw = 0
        self.E = {}
        for name, h in (("pe", nc.tensor), ("act", nc.scalar), ("dve", nc.vector),
                        ("pool", nc.gpsimd), ("sp", nc.sync)):
            self.E[name] = Eng(name, h, es.enter_context(nc.semaphore("s_" + name)))
        self.dsem = [es.enter_context(nc.semaphore(f"d{i}")) for i in range(N_DMA_SEMS)]
        self.dval = [0] * N_DMA_SEMS
        self.dnext = 0
        self.store_toks = []
        self.rec = None
        self.dry = False
        self.dummy_items = []
        self.n_dummy = 0
        self.trace = None
        self.free_t = {n: 0.0 for n in self.E}
        self.tok_t = {}
        self.busy = {}

    @staticmethod
    def _bufs(lst):
        out = []
        for x in lst:
            if isinstance(x, Buf):
                out.append(x)
            else:
                out.extend(x.bufs)
        return out

    def _wait(self, eng, tok):
        sem, val, _ = tok
        key = id(sem)
        if eng.waited.get(key, 0) >= val:
            return
        eng.h.wait_ge(sem, val)
        eng.waited[key] = val

    def _dep_toks(self, en, rb, wb, is_dma):
        toks = []
        for b in rb:
            if b.w is not None:
                if b.w[2] == en and en == "pe" and not is_dma:
                    continue
                toks.append(b.w)
        for b in wb:
            if b.w is not None:
                if not (b.w[2] == en and en == "pe" and not is_dma):
                    toks.append(b.w)
            for r in b.r:
                if r[2] == en and en == "pe" and not is_dma:
                    continue
                toks.append(r)
        return toks

    def _split(self, rd, wr):
        rb, wb = self._bufs(rd), self._bufs(wr)
        wb = wb + [b for b in rb if b.excl]
        rb = [b for b in rb if not b.excl]
        return rb, wb

    def _ready_t(self, toks):
        t = 0.0
        for tok in toks:
            t = max(t, self.tok_t.get((id(tok[0]), tok[1]), 0.0) + HOP)
        return t

    def peek(self, item):
        kind, en, fn, rd, wr, defer, c, kw = item
        rb, wb = self._split(rd, wr)
        toks = self._dep_toks(en, rb, wb, kind == "dma")
        return max(self.free_t[en], self._ready_t(toks))

    def acquire(self, name):
        if self.rec is not None:
            self.rec.append(("acq", name))

    def release(self, name):
        if self.rec is not None:
            self.rec.append(("rel", name))

    def mark(self, name):
        if self.rec is not None:
            self.rec.append(("mark", name))

    def wait_mark(self, name):
        if self.rec is not None:
            self.rec.append(("wait", name))

    def op(self, en, fn, rd=(), wr=(), defer=False, c=None):
        if self.rec is not None:
            p = _Proxy()
            fn(p)
            assert p.call is not None
            self.rec.append(("op", en, _replay(p.call), list(rd), list(wr), defer, c, None))
            return
        eng = self.E[en]
        rb, wb = self._split(rd, wr)
        for o in self.E.values():
            if o is not eng and (o.pend_r or o.pend_w):
                for b in wb:
                    assert all(b is not p for p in o.pend_r + o.pend_w), (en, o.name, b.name)
                for b in rb:
                    assert all(b is not p for p in o.pend_w), (en, o.name, b.name)
        toks = self._dep_toks(en, rb, wb, False)
        if not self.dry:
            for tok in toks:
                self._wait(eng, tok)
        start = max(self.free_t[en], self._ready_t(toks))
        end = start + (c if c is not None else DEF_COST[en]) * COST_SCALE[en]
        self.free_t[en] = end
        self.busy[en] = self.busy.get(en, 0.0) + end - start
        inst = None if self.dry else fn(eng.h)
        if defer:
            eng.pend_r += rb
            eng.pend_w += wb
            return
        eng.count += 1
        if not self.dry:
            inst.then_inc(eng.sem, 1)
        tok = (eng.sem, eng.count, en)
        self.tok_t[(id(eng.sem), eng.count)] = end
        rb += eng.pend_r
        wb += eng.pend_w
        eng.pend_r, eng.pend_w = [], []
        for b in wb:
            b.w = tok
            b.r = []
        for b in rb:
            b.r = [r for r in b.r if r[2] != en] + [tok]

    def dma(self, q, out, in_, rd=(), wr=(), store=False, c=None, occ=0.1, **kw):
        if self.rec is not None:
            self.rec.append(("dma", q, (out, in_, store, occ), list(rd), list(wr), False, c, kw))
            return
        eng = self.E[q]
        assert not eng.pend_r and not eng.pend_w
        rb, wb = self._bufs(rd), self._bufs(wr)
        for o in self.E.values():
            if o.pend_r or o.pend_w:
                for b in wb:
                    assert all(b is not p for p in o.pend_r + o.pend_w), ("dma", q, o.name, b.name)
                for b in rb:
                    assert all(b is not p for p in o.pend_w), ("dma", q, o.name, b.name)
        toks = self._dep_toks(q, rb, wb, True)
        if q == "pool":
            if self.dry:
                sem_, val_ = ("drysw", len(self.tok_t)), 16
            else:
                sem_, val_ = self.es.enter_context(self.nc.semaphore(f"sw{self.nsw}")), 16
                self.nsw += 1
                for tok in toks:
                    self._wait(eng, tok)
                eng.h.dma_start(out=out, in_=in_, **kw).then_inc(sem_, 16)
            tok = (sem_, val_, None)
        else:
            i = self.dnext
            self.dnext = (i + 1) % N_DMA_SEMS
            if not self.dry:
                for tok in toks:
                    self._wait(eng, tok)
                if self.dval[i]:
                    self._wait(eng, (self.dsem[i], self.dval[i], None))
            self.dval[i] += 16
            if not self.dry:
                eng.h.dma_start(out=out, in_=in_, **kw).then_inc(self.dsem[i], 16)
            tok = (self.dsem[i], self.dval[i], None)
        start = max(self.free_t[q], self._ready_t(toks))
        self.free_t[q] = start + occ
        self.tok_t[(id(tok[0]), tok[1])] = start + (c if c is not None else 3.0)
        for b in wb:
            b.w = tok
            b.r = []
        for b in rb:
            b.r.append(tok)
        if store:
            self.store_toks.append(tok)
        return tok

    def emit(self, item):
        if self.trace is not None:
            self.trace.append(item)
        kind, en, fn, rd, wr, defer, c, kw = item
        if kind == "op":
            self.op(en, fn, rd, wr, defer, c)
        else:
            out, in_, store, occ = fn
            self.dma(en, out, in_, rd, wr, store, c, occ, **kw)

    def record(self, gen):
        assert self.rec is None
        self.rec = []
        for _ in gen:
            pass
        lst, self.rec = self.rec, None
        return lst

    def run_streams(self, lists, use_sim=True):
        n = len(lists)
        idx = [0] * n
        held = {}
        marks = set()
        rr = 0
        while True:
            progressed = True
            while progressed:
                progressed = False
                for i in range(n):
                    while idx[i] < len(lists[i]) and lists[i][idx[i]][0] in PSEUDO:
                        kind, name = lists[i][idx[i]]
                        if kind == "acq":
                            if held.get(name, i) != i:
                                break
                            held[name] = i
                        elif kind == "rel":
                            assert held.get(name) == i, (name, i)
                            del held[name]
                        elif kind == "mark":
                            marks.add(name)
                        elif name not in marks:
                            break
                        idx[i] += 1
                        progressed = True
            cands = [i for i in range(n) if idx[i] < len(lists[i]) and lists[i][idx[i]][0] not in PSEUDO]
            if not cands:
                assert all(idx[i] >= len(lists[i]) for i in range(n)), ("lock deadlock", held, idx)
                return
            if use_sim:
                best = min(cands, key=lambda i: (self.peek(lists[i][idx[i]]) - (PRIO[i] if n == len(PRIO) else 0.0),
                                                 (i - rr) % n))
            else:
                best = cands[0]
            rr = (best + 1) % n
            self.emit(lists[best][idx[best]])
            idx[best] += 1

    def snapshot(self):
        return {
            "bufs": [(b, b.w, list(b.r)) for b in Buf.ALL],
            "eng": {n: (e.count, dict(e.waited), list(e.pend_r), list(e.pend_w)) for n, e in self.E.items()},
            "d": (list(self.dval), self.dnext), "free": dict(self.free_t), "tok": dict(self.tok_t),
            "busy": dict(self.busy), "stores": list(self.store_toks),
        }

    def restore(self, sn):
        for b, w, r in sn["bufs"]:
            b.w, b.r = w, list(r)
        for n, (cnt, wt, pr, pw) in sn["eng"].items():
            e = self.E[n]
            e.count, e.waited, e.pend_r, e.pend_w = cnt, dict(wt), list(pr), list(pw)
        self.dval, self.dnext = list(sn["d"][0]), sn["d"][1]
        self.free_t, self.tok_t, self.busy = dict(sn["free"]), dict(sn["tok"]), dict(sn["busy"])
        self.store_toks = list(sn["stores"])

    def run_dag(self, order, delta=0.3):
        groups, open_g = [], {}
        gof = []
        for it in order:
            kind, en = it[0], it[1]
            key = en if kind == "op" else None
            if key is not None and key in open_g:
                gi = open_g[key]
            else:
                gi = len(groups)
                groups.append({"items": [], "en": en, "cost": 0.0})
                if key is not None:
                    open_g[key] = gi
            g = groups[gi]
            g["items"].append(it)
            gof.append(gi)
            c = it[6]
            g["cost"] += (c if c is not None else DEF_COST[en]) if kind == "op" else 0.1
            if kind == "op" and not it[5]:
                open_g.pop(key, None)
        assert not open_g, "deferred ops without a covering op"
        n = len(groups)
        lastw, readers = {}, {}
        preds = [set() for _ in range(n)]
        for it, gi in zip(order, gof):
            rb, wb = self._split(it[3], it[4])
            for b in rb:
                if id(b) in lastw:
                    preds[gi].add(lastw[id(b)])
            for b in wb:
                if id(b) in lastw:
                    preds[gi].add(lastw[id(b)])
                for r in readers.get(id(b), ()):
                    preds[gi].add(r)
            for b in wb:
                lastw[id(b)] = gi
                readers[id(b)] = []
            for b in rb:
                readers.setdefault(id(b), []).append(gi)
        lastq = {}
        for gi, g in enumerate(groups):
            if g["items"][0][0] == "dma":
                q = g["en"]
                if q in lastq:
                    preds[gi].add(lastq[q])
                lastq[q] = gi
        for gi in range(n):
            preds[gi].discard(gi)
        succs = [[] for _ in range(n)]
        for gi in range(n):
            for p in preds[gi]:
                succs[p].append(gi)
        rem = [0.0] * n
        indeg = [len(p) for p in preds]
        topo = [gi for gi in range(n) if indeg[gi] == 0]
        for gi in topo:
            for s_ in succs[gi]:
                indeg[s_] -= 1
                if indeg[s_] == 0:
                    topo.append(s_)
        assert len(topo) == n, "cycle in dependency DAG"
        for gi in reversed(topo):
            m = 0.0
            for s_ in succs[gi]:
                m = max(m, rem[s_])
            rem[gi] = m + groups[gi]["cost"] + HOP
        npred = [len(p) for p in preds]
        started = set()
        ready = [gi for gi in range(n) if npred[gi] == 0]
        done = 0
        while ready:
            starts = [(self.peek(groups[gi]["items"][0]), gi) for gi in ready]
            tmin = min(s_ for s_, _ in starts)
            cands = [gi for s_, gi in starts if s_ <= tmin + delta]
            best = min(ready) if DAG_ORIG else max(cands, key=lambda gi: (rem[gi], -gi))
            ready.remove(best)
            if KEEPALIVE and groups[best]["en"] == "pe" and self.dummy_items:
                gap = self.peek(groups[best]["items"][0]) - self.free_t["pe"]
                if gap > KA_MIN_GAP:
                    budget = (gap - 0.15) * KA_FILL
                    usable = [d for d in self.dummy_items if id(self._bufs(d[4])[0]) in started]
                    while budget > KA_COST and usable:
                        cand = min(usable, key=self.peek)
                        if self.peek(cand) > self.free_t["pe"] + 0.05:
                            break
                        self.emit(cand)
                        self.n_dummy += 1
                        budget -= KA_COST
            for it in groups[best]["items"]:
                self.emit(it)
                if it[0] == "op" and it[1] == "pe":
                    for b in self._bufs(it[4]):
                        started.add(id(b))
            done += 1
            for s_ in succs[best]:
                npred[s_] -= 1
                if npred[s_] == 0:
                    ready.append(s_)
        assert done == n, (done, n)

    def finish(self):
        eng = self.E["sp"]
        for tok in self.store_toks:
            self._wait(eng, tok)
        for en in ("pe", "act", "dve", "pool"):
            e = self.E[en]
            assert not e.pend_r and not e.pend_w, en
            if e.count:
                self._wait(eng, (e.sem, e.count, en))


def build_program():
    nc = bass.Bass("TRN2", target_bir_lowering=False)

    def din(name, shape):
        return nc.dram_tensor(name, list(shape), F32, kind="ExternalInput").ap()

    def dout(name, shape):
        return nc.dram_tensor(name, list(shape), F32, kind="ExternalOutput").ap()

    xp = din("xp", [2, SEQ, D])
    xs = din("xs", [128, D])
    cak = din("cak", [4, 128, 128])
    cav = din("cav", [4, 128, 128])
    cbk = din("cbk", [4, 512, 512])
    cbv = din("cbv", [4, 512, 512])
    st4 = din("st4", [4, 4, D])
    vecs = din("vecs", [80, 128])
    lnpost = din("lnpost", [2, D])
    w_in = din("w_in", [D, 3328])
    w_out0 = din("w_out0", [D, D])
    w_inc = din("w_inc", [D, 2048])
    w_out1 = din("w_out1", [D, D])
    sinks = din("sinks", [8])
    tab = din("tab", [8, TABL])
    gwa = din("gwa", [16, 64, 64])
    gwx = din("gwx", [16, 64, 64])
    ropec = din("ropec", [128, 17, 32])
    ropes = din("ropes", [128, 17, 32])
    cmask = din("cmask", [128, 128])

    o_yp = dout("o_yp", [2, SEQ, D])
    o_ys = dout("o_ys", [128, D])
    o_akp = dout("o_akp", [2, 128, 128])
    o_avp = dout("o_avp", [2, 128, 128])
    o_bkp = dout("o_bkp", [2, 512, 512])
    o_bvp = dout("o_bvp", [2, 512, 512])
    o_chp = dout("o_chp", [2, D])
    o_ccp = dout("o_ccp", [2, 3, D])
    o_aks = dout("o_aks", [4, 128, 128])
    o_avs = dout("o_avs", [4, 128, 128])
    o_bks = dout("o_bks", [4, 512, 512])
    o_bvs = dout("o_bvs", [4, 512, 512])
    o_chs = dout("o_chs", [4, D])
    o_ccs = dout("o_ccs", [4, 3, D])

    es = ExitStack()
    with es:
        S = Sched(nc, es)
        op, dma = S.op, S.dma

        def sb(name, shape, dt=F32, nb=1):
            t = es.enter_context(nc.sbuf_tensor(name, list(shape), dt))
            return View(t, [Buf(name + str(i)) for i in range(nb)])

        def alias(v, dt, shape, bufs=None, off=0):
            t = v.t[:]
            if dt != F32:
                t = t.bitcast(dt)
            esz = 2 if dt == BF16 else 4
            n = int(np.prod(shape[1:]))
            o = off // esz
            t = t[:, o:o + n]
            if len(shape) == 3:
                t = t.rearrange("p (a b) -> p a b", a=shape[1])
            elif len(shape) == 4:
                t = t.rearrange("p (a b c) -> p a b c", a=shape[1], b=shape[2])
            return View(t, v.bufs if bufs is None else bufs)

        banks = [View(es.enter_context(nc.psum_tensor(f"bk{i}", [128, 512], F32)), [Buf(f"bk{i}", excl=True)])
                 for i in range(8)]
        IP = banks[0:2]
        TR = banks[2:4]
        ST = banks[4:6]
        PV = banks[6:8]

        def bk_bf(b):
            return b.t[:].bitcast(BF16)

        Win = sb("Win", [128, 8, 3328], BF16, nb=1)
        Wout0 = sb("Wout0", [128, 8, D], BF16)
        Winc = sb("Winc", [128, 8, 2048], BF16)
        Wout1 = sb("Wout1", [128, 8, D], BF16)
        BDa = sb("BDa", [128, 8, 128], BF16)
        BDx = sb("BDx", [128, 8, 128], BF16)
        GPOST = [sb(f"GPOST{l}", [128, D]) for l in range(2)]
        VT = sb("VT", [128, 80])
        ROPC = sb("ROPC", [128, 2, 32], nb=2)
        ROPS = sb("ROPS", [128, 2, 32], nb=2)
        EXS = sb("EXS", [128, 8])
        CLOG = sb("CLOG", [128, 8])
        IDB = sb("IDB", [128, 128], BF16)
        IDF = sb("IDF", [128, 128])
        JB = sb("JB", [128, 128], BF16)
        MSK = sb("MSK", [128, 128], BF16)
        BT = sb("BT", [128, 8, 4, 128], BF16)
        BSN = sb("BSN", [128, 8, 128], BF16)
        HALF = sb("HALF", [128, 1])

        X = [sb(f"X{i}", [128, D]) for i in range(2)]
        SSL = [sb(f"SS{l}", [128, 4]) for l in range(2)]
        hTL = [sb(f"hT{l}", [128, 8, 128], BF16) for l in range(2)]
        G1 = sb("G1", [128, 1024], F32, nb=4)
        G2 = sb("G2", [128, 1024], F32, nb=2)
        G3 = sb("G3", [128, 1024], F32, nb=2)
        G4 = sb("G4", [128, 1024], F32, nb=2)
        G5 = sb("G5", [128, 960], F32, nb=5)
        G6 = sb("G6", [128, 1024], F32, nb=2)
        G7 = sb("G7", [128, 1024], F32, nb=4)
        PTA = [alias(G1, BF16, [128, 4, 128], [G1.bufs[i]], off=1024 * i) for i in range(4)]
        T1 = alias(G2, F32, [128, 512], [G2.bufs[0]], off=0)
        T2 = alias(G2, F32, [128, 512], [G2.bufs[1]], off=2048)
        CBKb = alias(G2, BF16, [128, 4, 512])
        GTMP = alias(G3, F32, [128, 512], [G3.bufs[0]], off=0)
        GS = alias(G3, BF16, [128, 1024], [G3.bufs[1]], off=2048)
        Y = alias(G4, BF16, [128, 1024], [G4.bufs[0]] + [Buf(f"Yq{i}") for i in range(4)], off=0)
        YT = alias(G4, BF16, [128, 8, 128], [G4.bufs[1]], off=2048)
        QA = alias(G5, BF16, [128, 512], [G5.bufs[0]], off=0)
        KAf = alias(G5, F32, [128, 128], [G5.bufs[1]], off=1024)
        KAb = alias(G5, BF16, [128, 128], [G5.bufs[2]], off=1536)
        QB = alias(G5, BF16, [128, 512], [G5.bufs[3]], off=1792)
        KB = alias(G5, BF16, [128, 512], [G5.bufs[4]], off=2816)
        QAT = alias(G6, BF16, [128, 2, 4, 128], [G6.bufs[0]], off=0)
        QBT = alias(G6, BF16, [128, 4, 2, 128], [G6.bufs[1]], off=2048)
        PTB = [alias(G7, BF16, [128, 4, 128], [G7.bufs[i]], off=1024 * i) for i in range(4)]
        XN = alias(G4, F32, [128, D])
        KBf = alias(G3, F32, [128, 512], [G3.bufs[0]], off=0)
        VBf = alias(G2, F32, [128, 512], [G2.bufs[0]], off=0)
        STG = alias(G4, F32, [128, 8, 128])
        VAf = alias(G7, F32, [128, 128], [G7.bufs[0]], off=0)
        KAT = sb("KAT", [128, 2, 128], BF16, nb=2)
        KBT = sb("KBT", [128, 4, 5, 128], BF16, nb=5)
        VA = sb("VA", [128, 2, 2, 65], BF16, nb=2)
        VB = sb("VB", [128, 5, 8, 65], BF16, nb=5)
        CAKb = alias(G4, BF16, [128, 128], [G4.bufs[0]], off=0)
        PVT2 = [alias(G2, F32, [128, 4, 64], [G2.bufs[i]], off=2048 * i) for i in range(2)]
        DEN = sb("DEN", [128, 8], nb=2)
        class PSet:
            pass
        PS = []
        for i in range(2):
            p = PSet()
            p.XB = sb(f"pXB{i}", [128, 2 * 140])
            p.XBp = View(p.XB.t[:, 0:2 * 131].rearrange("p (c t) -> p c t", c=2), p.XB.bufs)
            p.XBs = View(p.XB.t[:].rearrange("p (c s t) -> p c s t", c=2, s=4), p.XB.bufs)
            p.XC = sb(f"pXC{i}", [128, 2, 128])
            p.XCb = sb(f"pXCb{i}", [128, 2, 128], BF16)
            p.LAB = sb(f"pLAB{i}", [128, 4, 128])
            p.LC = sb(f"pLC{i}", [128, 2, 128])
            p.HS = sb(f"pHS{i}", [128, 2, 128])
            p.ZS = sb(f"pZS{i}", [128, 2, 128], BF16)
            PS.append(p)
        pLAB = PS[0].LAB
        Y1T = View(BSN.t, [Buf(f"Y1T{q}") for q in range(4)] + BSN.bufs)
        HIST = sb("HIST", [128, 8, 3], nb=4)
        HST = sb("HST", [128, 8, 1], nb=4)
        ST4 = sb("ST4", [128, 8, 4, 4], nb=4)
        INIT = sb("INIT", [128, 8, 4, 4])
        S4 = View(pLAB.t[:].rearrange("p a t -> p (a t)"), pLAB.bufs)
        O4 = S4

        def ring(v, i):
            return View(v.t, [v.bufs[i]])

        op("pool", lambda e: e.memset(IDF[:], 1.0), wr=[IDF])
        op("pool", lambda e: e.affine_select(out=IDF[:], in_=IDF[:], pattern=[[-1, 128]],
                                             compare_op=ALU.is_equal, fill=0.0, base=0,
                                             channel_multiplier=1), rd=[IDF], wr=[IDF])
        op("pool", lambda e: e.tensor_copy(out=IDB[:], in_=IDF[:]), rd=[IDF], wr=[IDB])
        op("pool", lambda e: e.memset(T1[:, 0:128], 1.0), wr=[T1])
        op("pool", lambda e: e.affine_select(out=T1[:, 0:128], in_=T1[:, 0:128], pattern=[[1, 128]],
                                             compare_op=ALU.is_equal, fill=0.0, base=-127,
                                             channel_multiplier=1), rd=[T1], wr=[T1])
        op("pool", lambda e: e.tensor_copy(out=JB[:], in_=T1[:, 0:128]), rd=[T1], wr=[JB])
        op("pool", lambda e: e.memset(HALF[:], -0.5), wr=[HALF])
        dma("pool", MSK[:], cmask, wr=[MSK])
        op("pool", lambda e: e.memset(VA[:], 1.0), wr=[VA])
        op("pool", lambda e: e.memset(VB[:], 1.0), wr=[VB])
        op("pool", lambda e: e.memset(QAT[:], 0.0), wr=[QAT])
        op("pool", lambda e: e.memset(QBT[:], 0.0), wr=[QBT])
        op("pool", lambda e: e.memset(BDa[:], 0.0), wr=[BDa])
        op("pool", lambda e: e.memset(BDx[:], 0.0), wr=[BDx])
        dma("sp", T2[0:80, 0:128], vecs, wr=[T2])
        dma("sp", EXS[:], sinks.partition_broadcast(128), wr=[EXS])
        for l in range(2):
            dma("sp", GPOST[l][:], lnpost[l].partition_broadcast(128), wr=[GPOST[l]])
        op("pe", lambda e: e.transpose(out=TR[0][:, 0:80], in_=T2[0:80, 0:128], identity=IDF[0:80, 0:80]),
           rd=[T2, IDF], wr=[TR[0]])
        op("dve", lambda e: e.tensor_copy(out=VT[:], in_=TR[0][:, 0:80]), rd=[TR[0]], wr=[VT])
        VTv = View(VT.t[:].rearrange("p (v c) -> p v c", v=10), VT.bufs)
        op("act", lambda e: e.activation(out=EXS[:], in_=EXS[:], func=AF.Exp), rd=[EXS], wr=[EXS])
        op("act", lambda e: e.activation(out=CLOG[:], in_=VTv[:, 7, :], func=AF.Exp, scale=-1.0),
           rd=[VT], wr=[CLOG])
        op("act", lambda e: e.activation(out=CLOG[:], in_=CLOG[:], func=AF.Ln, bias=1.0, scale=1.0),
           rd=[CLOG], wr=[CLOG])
        op("dve", lambda e: e.tensor_scalar(out=CLOG[:], in0=CLOG[:], scalar1=-8.0, scalar2=None,
                                            op0=ALU.mult), rd=[CLOG], wr=[CLOG])

        def toep(off, npart, nq):
            return bass.AP(tensor=tab.tensor, offset=off, ap=[[1, npart], [TABL, 8], [1, nq]])
        for i, dt_ in enumerate((0, 1, 2, 4)):
            dma("sp", STG[:], toep(128 * dt_ + 1, 128, 128), wr=[STG])
            op("dve", lambda e, i=i: e.tensor_scalar(out=BT[:, :, i, :], in0=STG[:], scalar1=8.0,
                                                     scalar2=None, op0=ALU.mult), rd=[STG], wr=[BT])
        op("pool", lambda e: e.memset(BT[0:64, :, 0, 0:64], NEG), wr=[BT])
        op("pool", lambda e: e.memset(BT[64:128, :, 3, 64:128], NEG), wr=[BT])
        op("dve", lambda e: e.memset(STG[:], NEG / 8.0), wr=[STG])
        for s in range(4):
            dma("sp", STG[96 - 32 * s:128 - 32 * s, :, 32 * s:32 * s + 32], toep(97, 32, 32), wr=[STG])
        op("dve", lambda e: e.tensor_scalar(out=BSN[:], in0=STG[:], scalar1=8.0, scalar2=None,
                                            op0=ALU.mult), rd=[STG], wr=[BSN])

        def wload(dst, src, ncols):
            srcv = src.rearrange("(k p) n -> p k n", p=128)
            t_ = ncols / 512.0 * 1.35
            for k in range(8):
                dma("pool", dst[:, k, :], srcv[:, k, :], wr=[dst], max_dma_last_dim=4096, c=t_ + 2.0, occ=t_)
        wload(Win, w_in, 3328)

        def gen_weights():
            wload(Wout0, w_out0, D)
            wload(Winc, w_inc, 2048)
            for (bd, gw) in ((BDa, gwa), (BDx, gwx)):
                gv = gw.rearrange("(c two) i j -> two i c j", two=2)
                dma("pool", bd[0:64, :, 0:64], gv[0], wr=[bd], occ=1.0)
                dma("pool", bd[64:128, :, 64:128], gv[1], wr=[bd], occ=1.0)
            wload(Wout1, w_out1, D)
            yield

        def rstd_from(SS, n):
            if n == 2:
                op("dve", lambda e: e.tensor_tensor(out=SS[:, 2:3], in0=SS[:, 0:1], in1=SS[:, 1:2],
                                                    op=ALU.add), rd=[SS], wr=[SS])
                src = SS[:, 2:3]
            else:
                src = SS[:, 0:1]
            op("dve", lambda e: e.tensor_scalar(out=SS[:, 2:3], in0=src, scalar1=1.0 / D, scalar2=EPS,
                                                op0=ALU.mult, op1=ALU.add), rd=[SS], wr=[SS])
            op("pool", lambda e: e.tensor_tensor(out=SS[:, 3:4], in0=SS[:, 2:3], in1=HALF[:], op=ALU.pow),
               rd=[SS, HALF], wr=[SS])

        def prenorm(Xt, layer):
            SS = SSL[layer]
            hT = hTL[layer]
            if layer == 0:
                op("act", lambda e: e.activation(out=XN[:], in_=Xt[:], func=AF.Square, accum_out=SS[:, 0:1]),
                   rd=[Xt], wr=[XN, SS], c=0.85)
                rstd_from(SS, 1)
                op("dve", lambda e: e.tensor_scalar(out=XN[:], in0=Xt[:], scalar1=SS[:, 3:4], scalar2=None,
                                                    op0=ALU.mult), rd=[Xt, SS], wr=[XN], c=1.2)
            else:
                tmp = pLAB[:].rearrange("p a t -> p (a t)")
                for half in range(2):
                    op("act", lambda e, half=half: e.activation(
                        out=tmp, in_=Xt[:, half * 512:(half + 1) * 512], func=AF.Square,
                        accum_out=SS[:, half:half + 1]), rd=[Xt], wr=[pLAB, SS], c=0.6)
                rstd_from(SS, 2)
            for half in range(2):
                bk = TR[half]
                if layer == 1:
                    op("dve", lambda e, half=half: e.tensor_scalar(
                        out=tmp, in0=Xt[:, half * 512:(half + 1) * 512], scalar1=SS[:, 3:4], scalar2=None,
                        op0=ALU.mult), rd=[Xt, SS], wr=[pLAB], c=0.7)
                S.acquire(f"TR{half}")
                for k4 in range(4):
                    k = half * 4 + k4
                    if layer == 0:
                        src, srcv = XN[:, k * 128:(k + 1) * 128], XN
                    else:
                        src, srcv = tmp[:, k4 * 128:(k4 + 1) * 128], pLAB
                    op("pe", lambda e, k4=k4, bk=bk, src=src: e.transpose(
                        out=bk[:, k4 * 128:(k4 + 1) * 128], in_=src, identity=IDF[:]),
                        rd=[srcv, IDF], wr=[bk], defer=(k4 < 3), c=0.15)
                g = VTv[:, 8 + layer, half * 4:half * 4 + 4].unsqueeze(2).to_broadcast([128, 4, 128])
                op("dve", lambda e, bk=bk, half=half, g=g: e.tensor_tensor(
                    out=hT[:, half * 4:half * 4 + 4, :], in0=bk[:, :].rearrange("p (a b) -> p a b", a=4),
                    in1=g, op=ALU.mult), rd=[bk, VT], wr=[hT], c=0.7)
                S.release(f"TR{half}")

        def postnorm(Xt, layer):
            SS = SSL[layer]
            if layer == 0:
                tmp, tmpv = GTMP[:], GTMP
            else:
                tmp, tmpv = pLAB[:].rearrange("p a t -> p (a t)"), pLAB
            for n in range(2):
                op("act", lambda e, n=n: e.activation(out=tmp, in_=IP[n][:, :], func=AF.Square,
                                                      accum_out=SS[:, n:n + 1]), rd=[IP[n]], wr=[tmpv, SS], c=0.6)
            rstd_from(SS, 2)
            for n in range(2):
                op("dve", lambda e, n=n: e.scalar_tensor_tensor(
                    out=tmp, in0=IP[n][:, :], scalar=SS[:, 3:4], in1=GPOST[layer][:, n * 512:(n + 1) * 512],
                    op0=ALU.mult, op1=ALU.mult), rd=[IP[n], SS, GPOST[layer]], wr=[tmpv], c=0.7)
                op("pool", lambda e, n=n: e.tensor_tensor(
                    out=Xt[:, n * 512:(n + 1) * 512], in0=Xt[:, n * 512:(n + 1) * 512], in1=tmp, op=ALU.add),
                    rd=[Xt, tmpv], wr=[Xt], c=0.8)

        def sigmoid_inplace(ap, view):
            op("act", lambda e: e.activation(out=ap, in_=ap, func=AF.Exp, scale=-1.0), rd=[view], wr=[view])
            op("act", lambda e: e.activation(out=ap, in_=ap, func=AF.Ln, bias=1.0, scale=1.0), rd=[view], wr=[view])
            op("act", lambda e: e.activation(out=ap, in_=ap, func=AF.Exp, scale=-1.0), rd=[view], wr=[view])

        def silu_from(bank_ap, out_ap, out_view, bank, tmp, tmpv):
            op("act", lambda e: e.activation(out=tmp, in_=bank_ap, func=AF.Exp, scale=-1.0),
               rd=[bank], wr=[tmpv])
            op("act", lambda e: e.activation(out=tmp, in_=tmp, func=AF.Ln, bias=1.0, scale=1.0),
               rd=[tmpv], wr=[tmpv])
            op("act", lambda e: e.activation(out=tmp, in_=tmp, func=AF.Exp, scale=-1.0),
               rd=[tmpv], wr=[tmpv])
            op("dve", lambda e: e.tensor_tensor(out=out_ap, in0=bank_ap, in1=tmp, op=ALU.mult),
               rd=[bank, tmpv], wr=[out_view])

        def rope(bank, c0, nh, ti, out_ap, out_view):
            n = nh * 64
            src = bank[:, c0:c0 + n].rearrange("p (h t j) -> p h t j", h=nh, t=2)
            t1 = T1[:, 0:n].rearrange("p (h t j) -> p h t j", h=nh, t=2)
            t2 = T2[:, 0:n].rearrange("p (h t j) -> p h t j", h=nh, t=2)
            o = out_ap.rearrange("p (h t j) -> p h t j", h=nh, t=2)
            rcv, rsv = ring(ROPC, ti), ring(ROPS, ti)
            cb = ROPC[:, ti, :].unsqueeze(1).to_broadcast([128, nh, 32])
            sn = ROPS[:, ti, :].unsqueeze(1).to_broadcast([128, nh, 32])
            for t in range(2):
                op("dve", lambda e, t=t: e.tensor_tensor(out=t1[:, :, t, :], in0=src[:, :, t, :], in1=cb,
                                                         op=ALU.mult), rd=[bank, rcv], wr=[T1], defer=(t == 0))
            for t in range(2):
                op("dve", lambda e, t=t: e.tensor_tensor(out=t2[:, :, t, :], in0=src[:, :, t, :], in1=sn,
                                                         op=ALU.mult), rd=[bank, rsv], wr=[T2], defer=(t == 0))
            op("pool", lambda e: e.tensor_tensor(out=o[:, :, 0, :], in0=t1[:, :, 0, :], in1=t2[:, :, 1, :],
                                                 op=ALU.subtract), rd=[T1, T2], wr=[out_view])
            op("pool", lambda e: e.tensor_tensor(out=o[:, :, 1, :], in0=t1[:, :, 1, :], in1=t2[:, :, 0, :],
                                                 op=ALU.add), rd=[T1, T2], wr=[out_view])

        def transposes_bf(src_view, src_ap_fn, n, bank):
            bb = bk_bf(bank)
            for i in range(n):
                op("pe", lambda e, i=i: e.transpose(out=bb[:, i * 128:(i + 1) * 128], in_=src_ap_fn(i),
                                                    identity=IDB[:]),
                   rd=[src_view, IDB], wr=[bank], defer=(i < n - 1))
            return bb

        state = {"ptb": 0, "st": 0}

        def next_st():
            b = ST[state["st"]]
            state["st"] ^= 1
            return b

        def load_x(i):
            kind, seq, m = tiles[i]
            Xt = X[i % 2]
            ti = 0 if kind == "s" else 1 + m
            dma("sp", ROPC[:, i % 2, :], ropec[:, ti, :], wr=[ring(ROPC, i % 2)])
            dma("sp", ROPS[:, i % 2, :], ropes[:, ti, :], wr=[ring(ROPS, i % 2)])
            if kind == "s":
                dma("sp", Xt[:], xs, wr=[Xt])
            else:
                dma("sp", Xt[:], xp[seq, m * 128:(m + 1) * 128, :], wr=[Xt])

        allq = slice(0, 128)

        def run_steps(steps):
            n = len(steps)
            if n == 0:
                return
            steps[0]["qk"]()
            for i in range(n):
                steps[i]["exp"]()
                hoist = i + 1 < n and not steps[i + 1]["nohoist"]
                if hoist:
                    steps[i + 1]["qk"]()
                steps[i]["pv"]()
                if i + 1 < n and not hoist:
                    steps[i + 1]["qk"]()
                yield

        def gen_attn(tno, kind, seq, m, halves):
            sample = kind == "s"
            cnt = {"st": 0, "ptb": 0}

            def pick_st(g):
                if len(halves) == 1:
                    return ST[halves[0]]
                cnt["st"] ^= 1
                return ST[cnt["st"]]

            def pick_ptb(hg):
                if len(halves) == 1:
                    cnt["ptb"] ^= 1
                    return PTB[2 * halves[0] + cnt["ptb"]]
                cnt["ptb"] = (cnt["ptb"] + 1) % 4
                return PTB[cnt["ptb"]]

            pvfirst = {"A": [True, True], "B": [True, True]}

            def pv_mm(kind_, bankidx, col, lhsT, rhs, rd, last):
                bank = PV[bankidx]
                st_ = pvfirst[kind_][bankidx]
                pvfirst[kind_][bankidx] = False
                op("pe", lambda e: e.matmul(bank[:, col:col + 65], lhsT=lhsT, rhs=rhs, start=st_, stop=True,
                                            skip_group_check=True),
                   rd=rd, wr=[bank], defer=not last)

            def a_step(slot, g, qsl, nq, mask_block, use_msk, last, parity, pre=None):
                kar_ = ring(KAT, slot)
                var_ = ring(VA, slot)
                pt = PTA[2 * g + parity]
                box = {}

                def qk():
                    if pre is not None:
                        pre()
                    stb = pick_st(g)
                    box["stb"] = stb
                    stv = stb[:, :].rearrange("p (a b) -> p a b", a=4)
                    so = stv[:, :, qsl] if nq == 128 else stb[:, 0:4 * nq].rearrange("p (a b) -> p a b", a=4)
                    op("pe", lambda e: e.matmul(so, lhsT=KAT[:, slot, :], rhs=QAT[:, g, :, qsl],
                                                start=True, stop=True, skip_group_check=True),
                       rd=[kar_, QAT], wr=[stb])

                def ex():
                    stb = box["stb"]
                    stv = stb[:, :].rearrange("p (a b) -> p a b", a=4)
                    if nq < 128:
                        op("pool", lambda e: e.memset(pt[:], 0.0), wr=[pt])
                    if mask_block == "prev":
                        op("act", lambda e: e.activation(out=pt[0:64, :, 0:64], in_=stv[0:64, :, 0:64],
                                                         func=AF.Exp, scale=0.125), rd=[stb], wr=[pt], c=0.4)
                        op("act", lambda e: e.activation(out=pt[64:128, :, :], in_=stv[64:128, :, :],
                                                         func=AF.Exp, scale=0.125), rd=[stb], wr=[pt], c=0.6)
                    elif mask_block == "cur":
                        op("act", lambda e: e.activation(out=pt[0:64, :, :], in_=stv[0:64, :, :],
                                                         func=AF.Exp, scale=0.125), rd=[stb], wr=[pt], c=0.6)
                        op("act", lambda e: e.activation(out=pt[64:128, :, 64:128], in_=stv[64:128, :, 64:128],
                                                         func=AF.Exp, scale=0.125), rd=[stb], wr=[pt], c=0.4)
                    else:
                        so = stv[:, :, qsl] if nq == 128 else stb[:, 0:4 * nq].rearrange("p (a b) -> p a b", a=4)
                        op("act", lambda e: e.activation(out=pt[:, :, qsl], in_=so, func=AF.Exp,
                                                         scale=0.125), rd=[stb], wr=[pt])
                    if use_msk:
                        op("pool", lambda e: e.tensor_tensor(
                            out=pt[:], in0=pt[:], in1=MSK[:].unsqueeze(1).to_broadcast([128, 4, 128]),
                            op=ALU.mult), rd=[pt, MSK], wr=[pt])

                def pv():
                    for j in range(4):
                        h = 4 * g + j
                        pv_mm("A", h // 4, (h % 4) * 65, pt[:, j, :], VA[:, slot, g, :], [pt, var_],
                              last=(j == 3))
                return {"qk": qk, "exp": ex, "pv": pv, "nohoist": pre is not None}

            steps = []
            if sample:
                for s in range(4):
                    def pre(s=s):
                        dma("pool", CAKb[:], cak[s], wr=[CAKb])
                        dma("pool", VA[:, 1, :, 0:64], cav[s].rearrange("k (h d) -> k h d", h=2),
                            wr=[ring(VA, 1)])
                        S.acquire("TR0")
                        bb_ = transposes_bf(CAKb, lambda i: CAKb[:], 1, TR[0])
                        op("dve", lambda e: e.tensor_copy(out=KAT[:, 1, :], in_=bb_[:, 0:128]), rd=[TR[0]],
                           wr=[ring(KAT, 1)])
                        S.release("TR0")
                    for g in halves:
                        steps.append(a_step(1, g, slice(32 * s, 32 * s + 32), 32, None, False, False, s % 2,
                                            pre=pre if g == 0 else None))
                for g in halves:
                    steps.append(a_step(0, g, allq, 128, None, True, True, 0))
            else:
                if m > 0:
                    for g in halves:
                        steps.append(a_step((m - 1) % 2, g, allq, 128, "prev", False, False, 0))
                for g in halves:
                    steps.append(a_step(m % 2, g, allq, 128, "cur", False, True, 1))
            yield from run_steps(steps)

            def pv_finish(first_col, sink):
                for bi in halves:
                    bank = PV[bi]
                    PVT = PVT2[bi]
                    denv = ring(DEN, bi)
                    yv = View(Y.t, [Y.bufs[1 + (first_col + bi * 256) // 256]])
                    pvv = bank[:, 0:260].rearrange("p (h d) -> p h d", h=4)
                    dn = DEN[:, bi * 4:bi * 4 + 4]
                    if sink:
                        op("dve", lambda e: e.tensor_tensor(out=dn, in0=pvv[:, :, 64], in1=EXS[:, bi * 4:bi * 4 + 4],
                                                            op=ALU.add), rd=[bank, EXS], wr=[denv])
                        op("dve", lambda e: e.reciprocal(out=dn, in_=dn), rd=[denv], wr=[denv])
                    else:
                        op("dve", lambda e: e.reciprocal(out=dn, in_=pvv[:, :, 64]), rd=[bank], wr=[denv])
                    op("dve", lambda e: e.tensor_tensor(out=PVT[:], in0=pvv[:, :, 0:64],
                                                        in1=dn.unsqueeze(2).to_broadcast([128, 4, 64]),
                                                        op=ALU.mult), rd=[bank, denv], wr=[PVT])
                    c0 = first_col + bi * 256
                    op("pool", lambda e: e.tensor_tensor(
                        out=Y[:, c0:c0 + 256], in0=PVT[:].rearrange("p h d -> p (h d)"), in1=GS[:, c0:c0 + 256],
                        op=ALU.mult), rd=[PVT, GS], wr=[yv])

            pv_finish(0, True)
            yield

            def b_step(slot, hg, qsl, nq, bias_fn, pre=None):
                kbr_ = ring(KBT, slot)
                vbr_ = ring(VB, slot)
                box = {}

                def qk():
                    if pre is not None:
                        pre()
                    stb = pick_st(hg)
                    box["stb"] = stb
                    stv = stb[:, :].rearrange("p (a b) -> p a b", a=4)
                    so = stv[:, :, qsl] if nq == 128 else stb[:, 0:4 * nq].rearrange("p (a b) -> p a b", a=4)
                    for jp in range(2):
                        c = 2 * hg + jp
                        op("pe", lambda e: e.matmul(so[:, 2 * jp:2 * jp + 2, :], lhsT=KBT[:, c, slot, :],
                                                    rhs=QBT[:, c, :, qsl],
                                                    start=(jp == 0), stop=False, skip_group_check=True),
                           rd=[kbr_, QBT], wr=[stb], defer=True, c=0.2 if nq == 128 else 0.1)
                    brhs, bview = bias_fn(hg)
                    op("pe", lambda e: e.matmul(so, lhsT=JB[:], rhs=brhs, start=False, stop=True,
                                                skip_group_check=True),
                       rd=[JB, bview], wr=[stb], c=0.34 if nq == 128 else 0.12)

                def ex():
                    stb = box["stb"]
                    stv = stb[:, :].rearrange("p (a b) -> p a b", a=4)
                    pt = pick_ptb(hg)
                    box["pt"] = pt
                    if nq < 128:
                        op("pool", lambda e: e.memset(pt[:], 0.0), wr=[pt])
                    so = stv[:, :, qsl] if nq == 128 else stb[:, 0:4 * nq].rearrange("p (a b) -> p a b", a=4)
                    op("act", lambda e: e.activation(out=pt[:, :, qsl], in_=so, func=AF.Exp,
                                                     scale=0.125), rd=[stb], wr=[pt])

                def pv():
                    pt = box["pt"]
                    for j in range(4):
                        h = 4 * hg + j
                        pv_mm("B", hg, j * 65, pt[:, j, :], VB[:, slot, h, :], [pt, vbr_], last=(j == 3))
                return {"qk": qk, "exp": ex, "pv": pv, "nohoist": pre is not None}

            steps = []
            if sample:
                for s in range(4):
                    def pre(s=s):
                        dma("pool", CBKb[:], cbk[s].rearrange("(t k) f -> k t f", k=128), wr=[CBKb])
                        for t in range(4):
                            dma("pool", VB[:, t, :, 0:64],
                                cbv[s, t * 128:(t + 1) * 128, :].rearrange("k (h d) -> k h d", h=8),
                                wr=[ring(VB, t)])
                        for t in range(4):
                            S.acquire(f"TR{t % 2}")
                            bb_ = transposes_bf(CBKb, lambda i, t=t: CBKb[:, t, i * 128:(i + 1) * 128], 4, TR[t % 2])
                            op("dve", lambda e, t=t, bb_=bb_: e.tensor_copy(
                                out=KBT[:, :, t, :], in_=bb_[:, 0:512].rearrange("p (a b) -> p a b", a=4)),
                                rd=[TR[t % 2]], wr=[ring(KBT, t)])
                            S.release(f"TR{t % 2}")
                    qsl = slice(32 * s, 32 * s + 32)
                    for t in range(4):
                        bi_ = 2 if t < 3 else 1
                        for hg in halves:
                            steps.append(b_step(t, hg, qsl, 32,
                                                lambda hg_, bi_=bi_: (BT[:, 4 * hg_:4 * hg_ + 4, bi_, 0:32], BT),
                                                pre=pre if (t == 0 and hg == 0) else None))
                for hg in halves:
                    steps.append(b_step(4, hg, allq, 128, lambda hg_: (BSN[:, 4 * hg_:4 * hg_ + 4, :], BSN)))
            else:
                dts = [d for d in (4, 3, 2, 1, 0) if m - d >= 0]
                for d in dts:
                    bi_ = {0: 0, 1: 1, 2: 2, 3: 2, 4: 3}[d]
                    for hg in halves:
                        steps.append(b_step((m - d) % 5, hg, allq, 128,
                                            lambda hg_, bi_=bi_: (BT[:, 4 * hg_:4 * hg_ + 4, bi_, :], BT)))
            yield from run_steps(steps)
            pv_finish(512, False)
            yield


        def gen_L0(tno, kind, seq, m):
            sample = kind == "s"
            ti = tno % 2
            Xt = X[tno % 2]
            slotA = 0 if sample else m % 2
            slotB = 4 if sample else m % 5
            out_a = (not sample) and m == NT - 1
            out_b = sample or m >= NT - 4

            prenorm(Xt, 0)
            hT = hTL[0]
            yield

            def inproj(cg, ncols):
                bank = IP[cg % 2]
                S.acquire(f"IP{cg % 2}")
                for k in range(8):
                    op("pe", lambda e, k=k: e.matmul(bank[:, 0:ncols], lhsT=hT[:, k, :],
                                                     rhs=Win[:, k, cg * 512:cg * 512 + ncols],
                                                     start=(k == 0), stop=(k == 7)),
                       rd=[hT, Win], wr=[bank], defer=(k < 7), c=0.34 if ncols == 512 else 0.2)
                return bank

            bank = inproj(0, 512)
            rope(bank, 0, 8, ti, QA[:], QA)
            S.release("IP0")
            yield
            bank = inproj(1, 512)
            op("act", lambda e: e.copy(out=QB[:], in_=bank[:, :]), rd=[bank], wr=[QB])
            S.release("IP1")
            S.acquire("TR0")
            bb = transposes_bf(QA, lambda i: QA[:, i * 128:(i + 1) * 128], 4, TR[0])
            bbv = bb[:, 0:512].rearrange("p (a b) -> p a b", a=4)
            op("dve", lambda e: e.tensor_copy(out=QAT[0:64, 0, :, :], in_=bbv[0:64]), rd=[TR[0]], wr=[QAT])
            op("act", lambda e: e.copy(out=QAT[64:128, 1, :, :], in_=bbv[64:128]), rd=[TR[0]], wr=[QAT])
            S.release("TR0")
            yield
            bank = inproj(2, 512)
            op("act", lambda e: e.copy(out=KB[:], in_=bank[:, :]), rd=[bank], wr=[KB])
            if out_b:
                op("dve", lambda e: e.tensor_copy(out=KBf[:], in_=bank[:, :]), rd=[bank], wr=[KBf])
            S.release("IP0")
            S.acquire("TR1")
            bb = transposes_bf(QB, lambda i: QB[:, i * 128:(i + 1) * 128], 4, TR[1])
            bbv = bb[:, 0:512].rearrange("p (a b) -> p a b", a=4)
            op("dve", lambda e: e.tensor_copy(out=QBT[0:64, :, 0, :], in_=bbv[0:64]), rd=[TR[1]], wr=[QBT])
            op("act", lambda e: e.copy(out=QBT[64:128, :, 1, :], in_=bbv[64:128]), rd=[TR[1]], wr=[QBT])
            S.release("TR1")
            yield
            bank = inproj(3, 512)
            vbr = ring(VB, slotB)
            op("dve", lambda e: e.tensor_copy(out=VB[:, slotB, :, 0:64],
                                              in_=bank[:, :].rearrange("p (h d) -> p h d", h=8)),
               rd=[bank], wr=[vbr])
            if out_b:
                op("act", lambda e: e.copy(out=VBf[:], in_=bank[:, :]), rd=[bank], wr=[VBf])
                if sample:
                    for s in range(4):
                        r0 = 32 * s
                        dma("sp", o_bks[s, 480:512, :], KBf[r0:r0 + 32, :], rd=[KBf], store=True)
                        dma("sp", o_bvs[s, 480:512, :], VBf[r0:r0 + 32, :], rd=[VBf], store=True)
                else:
                    r0 = (m - (NT - 4)) * 128
                    dma("sp", o_bkp[seq, r0:r0 + 128, :], KBf[:], rd=[KBf], store=True)
                    dma("sp", o_bvp[seq, r0:r0 + 128, :], VBf[:], rd=[VBf], store=True)
            S.release("IP1")
            S.acquire("TR0")
            bb = transposes_bf(KB, lambda i: KB[:, i * 128:(i + 1) * 128], 4, TR[0])
            kbr = ring(KBT, slotB)
            op("dve", lambda e: e.tensor_copy(out=KBT[:, :, slotB, :],
                                              in_=bb[:, 0:512].rearrange("p (a b) -> p a b", a=4)),
               rd=[TR[0]], wr=[kbr])
            S.release("TR0")
            yield
            bank = inproj(6, 256)
            rope(bank, 0, 2, ti, KAf[:], KAf)
            op("pool", lambda e: e.tensor_copy(out=KAb[:], in_=KAf[:]), rd=[KAf], wr=[KAb])
            var = ring(VA, slotA)
            op("dve", lambda e: e.tensor_copy(out=VA[:, slotA, :, 0:64],
                                              in_=bank[:, 128:256].rearrange("p (h d) -> p h d", h=2)),
               rd=[bank], wr=[var])
            if out_a or sample:
                op("act", lambda e: e.copy(out=VAf[:], in_=bank[:, 128:256]), rd=[bank], wr=[VAf])
            S.release("IP0")
            yield
            for cg in (4, 5):
                bank = inproj(cg, 512)
                silu_from(bank[:, :], GS[:, (cg - 4) * 512:(cg - 3) * 512], GS, bank, GTMP[:], GTMP)
                S.release(f"IP{cg % 2}")
                if cg == 4:
                    S.acquire("TR1")
                    bb = transposes_bf(KAb, lambda i: KAb[:], 1, TR[1])
                    kar = ring(KAT, slotA)
                    op("dve", lambda e: e.tensor_copy(out=KAT[:, slotA, :], in_=bb[:, 0:128]), rd=[TR[1]],
                       wr=[kar])
                    S.release("TR1")
                yield

            if sample:
                for s in range(4):
                    r0 = 32 * s
                    dma("sp", o_aks[s, 96:128, :], KAf[r0:r0 + 32, :], rd=[KAf], store=True)
                    dma("sp", o_avs[s, 96:128, :], VAf[r0:r0 + 32, :], rd=[VAf], store=True)
                    dma("sp", o_aks[s, 0:96, :], cak[s, 32:128, :], store=True)
                    dma("sp", o_avs[s, 0:96, :], cav[s, 32:128, :], store=True)
                    dma("sp", o_bks[s, 0:480, :], cbk[s, 32:512, :], store=True)
                    dma("sp", o_bvs[s, 0:480, :], cbv[s, 32:512, :], store=True)
            else:
                if out_a:
                    dma("sp", o_akp[seq], KAf[:], rd=[KAf], store=True)
                    dma("sp", o_avp[seq], VAf[:], rd=[VAf], store=True)

            split = not sample
            if tno == 1:
                for pt_ in PTA:
                    op("pool", lambda e, pt_=pt_: e.memset(pt_[:], 0.0), wr=[pt_])
            if split:
                S.mark(f"attn{tno}")
                yield from gen_attn(tno, kind, seq, m, (0,))
                S.wait_mark(f"Ydone{tno}")
            else:
                yield from gen_attn(tno, kind, seq, m, (0, 1))
            S.acquire("TR0")
            bb0 = transposes_bf(Y, lambda i: Y[:, i * 128:(i + 1) * 128], 4, TR[0])
            op("dve", lambda e: e.tensor_copy(out=YT[:, 0:4, :], in_=bb0[:, 0:512].rearrange("p (a b) -> p a b", a=4)),
               rd=[TR[0]], wr=[YT])
            S.release("TR0")

            S.acquire("TR1")
            bb1 = transposes_bf(Y, lambda i: Y[:, 512 + i * 128:512 + (i + 1) * 128], 4, TR[1])
            op("act", lambda e: e.copy(out=YT[:, 4:8, :], in_=bb1[:, 0:512].rearrange("p (a b) -> p a b", a=4)),
               rd=[TR[1]], wr=[YT])
            S.release("TR1")
            S.acquire("IP0")
            S.acquire("IP1")
            for n in range(2):
                for k in range(8):
                    op("pe", lambda e, k=k, n=n: e.matmul(IP[n][:, :], lhsT=YT[:, k, :],
                                                          rhs=Wout0[:, k, n * 512:(n + 1) * 512],
                                                          start=(k == 0), stop=(k == 7)),
                       rd=[YT, Wout0], wr=[IP[n]], defer=(k < 7), c=0.34)
            postnorm(Xt, 0)
            S.release("IP0")
            S.release("IP1")
            yield

        def gen_L1_pre(tno, kind, seq, m):
            sample = kind == "s"
            Xt = X[tno % 2]
            prenorm(Xt, 1)
            if sample:
                S.acquire("TR0")
                for s in range(4):
                    for half in range(2):
                        dma("sp", S4[0:4, :], st4[s, :, half * 512:(half + 1) * 512], wr=[S4])
                        for c4 in range(4):
                            c = half * 4 + c4
                            op("pe", lambda e, c=c, c4=c4: e.transpose(out=TR[0][:, c * 4:c * 4 + 4],
                                                                       in_=S4[0:4, c4 * 128:(c4 + 1) * 128],
                                                                       identity=IDF[0:4, 0:4]),
                               rd=[S4, IDF], wr=[TR[0]], defer=(c4 < 3))
                    op("dve", lambda e, s=s: e.tensor_copy(
                        out=INIT[:, :, s, :], in_=TR[0][:, 0:32].rearrange("p (c f) -> p c f", c=8)),
                        rd=[TR[0]], wr=[INIT])
                S.release("TR0")
            elif m == 0:
                op("pool", lambda e: e.memset(HIST[:], 0.0), wr=[HIST])
                op("pool", lambda e: e.memset(HST[:], 0.0), wr=[HST])
            yield

        def gen_L1_pieces(tno, kind, seq, m, qs, P):
            sample = kind == "s"
            hT = hTL[1]
            need_state = sample or m == NT - 1
            pXBs, pXBp, pXB, pXC, pXCb, pLAB, pLC, pHS, pZS = P.XBs, P.XBp, P.XB, P.XC, P.XCb, P.LAB, P.LC, P.HS, P.ZS
            for q in qs:
                c0 = 2 * q
                hist, hst, st4v, y1t = ring(HIST, q), ring(HST, q), ring(ST4, q), ring(Y1T, q)
                bank = IP[q % 2]
                S.acquire(f"IP{q % 2}")
                ocs = (c0, c0 + 1, 8 + c0, 9 + c0)
                for i, oc in enumerate(ocs):
                    for k in range(8):
                        op("pe", lambda e, k=k, oc=oc, i=i: e.matmul(
                            bank[:, i * 128:(i + 1) * 128], lhsT=Winc[:, k, oc * 128:(oc + 1) * 128],
                            rhs=hT[:, k, :], start=(k == 0 and i == 0), stop=(k == 7),
                            skip_group_check=True),
                            rd=[Winc, hT], wr=[bank], defer=not (k == 7 and i == 3))
                if sample:
                    op("pool", lambda e: e.tensor_copy(out=pXBs[:, :, :, 0:3], in_=INIT[:, c0:c0 + 2, :, 1:4]),
                       rd=[INIT], wr=[pXB])
                    op("dve", lambda e: e.tensor_copy(
                        out=pXBs[:, :, :, 3:35], in_=bank[:, 0:256].rearrange("p (a s t) -> p a s t", a=2, s=4)),
                        rd=[bank], wr=[pXB])
                else:
                    op("pool", lambda e: e.tensor_copy(out=pXBp[:, :, 0:3], in_=HIST[:, c0:c0 + 2, :]),
                       rd=[hist], wr=[pXB])
                    op("dve", lambda e: e.tensor_copy(
                        out=pXBp[:, :, 3:131], in_=bank[:, 0:256].rearrange("p (a t) -> p a t", a=2)),
                        rd=[bank], wr=[pXB])
                silu_from(bank[:, 256:512].rearrange("p (a t) -> p a t", a=2), pZS[:], pZS, bank, pLC[:], pLC)
                S.release(f"IP{q % 2}")
                yield
                for j in range(2):
                    c = c0 + j
                    if sample:
                        tap_ap = lambda tap, j=j: pXBs[:, j, :, tap:tap + 32]
                        xc_ap = pXC[:, j, :].rearrange("p (s t) -> p s t", s=4)
                    else:
                        tap_ap = lambda tap, j=j: pXBp[:, j, tap:tap + 128]
                        xc_ap = pXC[:, j, :]
                    op("dve", lambda e, c=c, xc_ap=xc_ap, tap_ap=tap_ap: e.tensor_scalar(
                        out=xc_ap, in0=tap_ap(3), scalar1=VTv[:, 3, c:c + 1], scalar2=VTv[:, 4, c:c + 1],
                        op0=ALU.mult, op1=ALU.add), rd=[pXB, VT], wr=[pXC], c=0.32)
                    for tap in (2, 1, 0):
                        op("dve", lambda e, c=c, tap=tap, xc_ap=xc_ap, tap_ap=tap_ap: e.scalar_tensor_tensor(
                            out=xc_ap, in0=tap_ap(tap), scalar=VTv[:, tap, c:c + 1], in1=xc_ap,
                            op0=ALU.mult, op1=ALU.add), rd=[pXB, VT, pXC], wr=[pXC], c=0.44)
                op("act", lambda e: e.copy(out=pXCb[:], in_=pXC[:]), rd=[pXC], wr=[pXCb], c=0.4)
                if not sample:
                    op("pool", lambda e: e.tensor_copy(out=HIST[:, c0:c0 + 2, :], in_=pXBp[:, :, 128:131]),
                       rd=[pXB], wr=[hist], c=0.25)
                if need_state:
                    if sample:
                        op("pool", lambda e: e.tensor_copy(out=ST4[:, c0:c0 + 2, :, 1:4], in_=pXBs[:, :, :, 32:35]),
                           rd=[pXB], wr=[st4v])
                    else:
                        op("pool", lambda e: e.tensor_copy(out=ST4[:, c0:c0 + 2, 0, 1:4], in_=pXBp[:, :, 128:131]),
                           rd=[pXB], wr=[st4v])
                yield
                gb = TR[q % 2]
                S.acquire(f"TR{q % 2}")
                for i in range(4):
                    bd = BDa if i < 2 else BDx
                    j = i % 2
                    op("pe", lambda e, i=i, j=j, bd=bd: e.matmul(
                        gb[:, i * 128:(i + 1) * 128], lhsT=bd[:, c0 + j, :], rhs=pXCb[:, j, :],
                        start=(i == 0), stop=True, skip_group_check=True),
                        rd=[bd, pXCb], wr=[gb], defer=(i < 3))
                bias4 = VTv[:, 5:7, c0:c0 + 2].unsqueeze(3).to_broadcast([128, 2, 2, 128])
                op("dve", lambda e: e.tensor_tensor(
                    out=pLAB[:].rearrange("p (v c) t -> p v c t", v=2),
                    in0=gb[:, :].rearrange("p (v c t) -> p v c t", v=2, c=2), in1=bias4, op=ALU.add),
                    rd=[gb, VT], wr=[pLAB], c=0.7)
                S.release(f"TR{q % 2}")
                sigmoid_inplace(pLAB[:], pLAB)
                yield
                for j in range(2):
                    op("act", lambda e, j=j: e.activation(out=pLAB[:, j, :], in_=pLAB[:, j, :], func=AF.Exp,
                                                          scale=CLOG[:, c0 + j:c0 + j + 1]),
                       rd=[pLAB, CLOG], wr=[pLAB], c=0.4)
                op("pool", lambda e: e.tensor_tensor(out=pLAB[:, 2:4, :], in0=pLAB[:, 2:4, :], in1=pXC[:],
                                                     op=ALU.mult), rd=[pLAB, pXC], wr=[pLAB], c=0.7)
                op("dve", lambda e: e.scalar_tensor_tensor(
                    out=pLC[:].rearrange("p c t -> p (c t)"), in0=pLAB[:, 0:2, :].rearrange("p c t -> p (c t)"),
                    scalar=-1.0, in1=pLAB[:, 0:2, :].rearrange("p c t -> p (c t)"), op0=ALU.mult, op1=ALU.mult),
                    rd=[pLAB], wr=[pLC], c=0.45)
                op("act", lambda e: e.activation(out=pLC[:], in_=pLC[:], func=AF.Ln, bias=1.0, scale=1.0),
                   rd=[pLC], wr=[pLC], c=0.42)
                op("act", lambda e: e.activation(out=pLC[:], in_=pLC[:], func=AF.Exp, scale=0.5),
                   rd=[pLC], wr=[pLC], c=0.42)
                op("pool", lambda e: e.tensor_tensor(out=pLAB[:, 2:4, :], in0=pLAB[:, 2:4, :], in1=pLC[:],
                                                     op=ALU.mult), rd=[pLAB, pLC], wr=[pLAB], c=0.7)
                yield
                if sample:
                    for j in range(2):
                        for s in range(4):
                            sl = slice(32 * s, 32 * s + 32)
                            op("dve", lambda e, j=j, s=s, sl=sl: e.tensor_tensor_scan(
                                out=pHS[:, j, sl], data0=pLAB[:, j, sl], data1=pLAB[:, 2 + j, sl],
                                initial=INIT[:, c0 + j, s, 0:1], op0=ALU.mult, op1=ALU.add),
                                rd=[pLAB, INIT], wr=[pHS], defer=not (j == 1 and s == 3), c=0.25)
                    op("pool", lambda e: e.tensor_copy(
                        out=ST4[:, c0:c0 + 2, :, 0],
                        in_=pHS[:].rearrange("p c (s t) -> p c s t", s=4)[:, :, :, 31]), rd=[pHS], wr=[st4v])
                else:
                    for j in range(2):
                        op("dve", lambda e, j=j: e.tensor_tensor_scan(
                            out=pHS[:, j, :], data0=pLAB[:, j, :], data1=pLAB[:, 2 + j, :],
                            initial=HST[:, c0 + j, 0:1], op0=ALU.mult, op1=ALU.add),
                            rd=[pLAB, hst], wr=[pHS], defer=(j == 0), c=0.44)
                    op("dve", lambda e: e.tensor_copy(out=HST[:, c0:c0 + 2, 0], in_=pHS[:, :, 127]),
                       rd=[pHS], wr=[hst], c=0.2)
                    if need_state:
                        op("pool", lambda e: e.tensor_copy(out=ST4[:, c0:c0 + 2, 0, 0], in_=pHS[:, :, 127]),
                           rd=[pHS], wr=[st4v])
                op("pool", lambda e: e.tensor_tensor(out=Y1T[:, c0:c0 + 2, :], in0=pHS[:], in1=pZS[:], op=ALU.mult),
                   rd=[pHS, pZS], wr=[y1t], c=0.7)
                yield

        def gen_L1_post(tno, kind, seq, m):
            sample = kind == "s"
            Xt = X[tno % 2]

            def state_out(s, dst_h, dst_c):
                S.acquire("TR1")
                for half in range(2):
                    for c4 in range(4):
                        c = half * 4 + c4
                        op("pe", lambda e, c=c, c4=c4: e.transpose(out=TR[1][0:4, c4 * 128:(c4 + 1) * 128],
                                                                   in_=ST4[:, c, s, :], identity=IDF[:]),
                           rd=[ST4, IDF], wr=[TR[1]], defer=(c4 < 3))
                    op("dve", lambda e: e.tensor_copy(out=O4[0:4, :], in_=TR[1][0:4, :]), rd=[TR[1]], wr=[O4])
                    dma("sp", dst_h[:, half * 512:(half + 1) * 512], O4[0:1, :], rd=[O4], store=True)
                    dma("sp", dst_c[:, half * 512:(half + 1) * 512], O4[1:4, :], rd=[O4], store=True)
                S.release("TR1")

            if sample:
                for s in range(4):
                    state_out(s, o_chs[s:s + 1, :], o_ccs[s])
            elif m == NT - 1:
                state_out(0, o_chp[seq:seq + 1, :], o_ccp[seq])

            S.acquire("IP0")
            S.acquire("IP1")
            for n in range(2):
                for k in range(8):
                    op("pe", lambda e, k=k, n=n: e.matmul(IP[n][:, :], lhsT=Y1T[:, k, :],
                                                          rhs=Wout1[:, k, n * 512:(n + 1) * 512],
                                                          start=(k == 0), stop=(k == 7)),
                       rd=[Y1T, Wout1], wr=[IP[n]], defer=(k < 7), c=0.34)
            postnorm(Xt, 1)
            S.release("IP0")
            S.release("IP1")
            if sample:
                dma("sp", o_ys, Xt[:], rd=[Xt], store=True)
            else:
                dma("sp", o_yp[seq, m * 128:(m + 1) * 128, :], Xt[:], rd=[Xt], store=True)
            yield

        tiles = [("s", 0, 0)] + [("p", seq, m) for seq in range(2) for m in range(NT)]
        if NTILES_DEBUG is not None:
            tiles = tiles[:NTILES_DEBUG]
        nt_ = len(tiles)
        for bi in range(2):
            call = ("matmul", (PV[bi][:, 264:512],),
                    dict(lhsT=IDB[:], rhs=Wout0[:, 0, 0:248], start=False, stop=True, skip_group_check=True))
            S.dummy_items.append(("op", "pe", _replay(call), [IDB, Wout0], [PV[bi]], False, KA_COST, None))

        def stream_A():
            for i, t in enumerate(tiles):
                S.wait_mark(f"L0done{i}")
                yield from gen_L1_pre(i, *t)
                S.mark(f"pre{i}")
                yield from gen_L1_pieces(i, *t, (0, 1), PS[0])
                S.wait_mark(f"Bdone{i}")
                yield from gen_L1_post(i, *t)
                S.mark(f"L1done{i}")

        def stream_B():
            for i, t in enumerate(tiles):
                S.wait_mark(f"pre{i}")
                yield from gen_L1_pieces(i, *t, (2, 3), PS[1])
                S.mark(f"Bdone{i}")

        def stream_C():
            for i, t in enumerate(tiles):
                if i >= 2:
                    S.wait_mark(f"L1done{i - 2}")
                load_x(i)
                yield from gen_L0(i, *t)
                S.mark(f"L0done{i}")

        def stream_D():
            for i, t in enumerate(tiles):
                if t[0] == "s":
                    continue
                S.wait_mark(f"attn{i}")
                yield from gen_attn(i, *t, (1,))
                S.mark(f"Ydone{i}")

        if PIPELINE:
            lists = [S.record(stream_A()), S.record(stream_B()), S.record(stream_C()), S.record(stream_D()),
                     S.record(gen_weights())]
            if DAG:
                snap = S.snapshot()
                S.dry, S.trace = True, []
                S.run_streams(lists)
                order, S.trace, S.dry = S.trace, None, False
                t_streams = max(S.free_t.values())
                S.restore(snap)
                if DAG_ORIG == 2:
                    for it_ in order:
                        S.emit(it_)
                else:
                    S.run_dag(order, DAG_DELTA)
                print("stream-order makespan %.1f -> dag makespan %.1f, keepalive matmuls %d" % (
                    t_streams, max(S.free_t.values()), S.n_dummy))
            else:
                S.run_streams(lists)
        else:
            for i, t in enumerate(tiles):
                if i == 0:
                    S.run_streams([S.record(gen_weights())])
                load_x(i)
                if t[0] == "s":
                    S.run_streams([S.record(gen_L0(i, *t))])
                else:
                    def d_(i=i, t=t):
                        S.wait_mark(f"attn{i}")
                        yield from gen_attn(i, *t, (1,))
                        S.mark(f"Ydone{i}")
                    S.run_streams([S.record(gen_L0(i, *t)), S.record(d_())])
                S.run_streams([S.record(gen_L1_pre(i, *t))])
                S.run_streams([S.record(gen_L1_pieces(i, *t, (0, 1, 2, 3), PS[0]))])
                S.run_streams([S.record(gen_L1_post(i, *t))])
        S.finish()
        print("simulated makespan (us):", round(max(S.free_t.values()), 1), {k: round(v) for k, v in S.busy.items()})
    return nc


_CACHE = {}


def _rope_tables():
    half = 32
    inv = 10000.0 ** (-np.arange(half, dtype=np.float64) / half)
    pos = np.zeros((128, 17), np.float64)
    p = np.arange(128)
    pos[:, 0] = 1024 + (p % 32)
    for i in range(16):
        pos[:, 1 + i] = i * 128 + p
    ang = pos[:, :, None] * inv[None, None, :]
    return np.cos(ang).astype(np.float32), np.sin(ang).astype(np.float32)


def kernel(x_prompt, x_sample, cache_a_k, cache_a_v, cache_b_k, cache_b_v, state_c_h, state_c_conv,
           ln_pre, ln_post, w_in_ab, sinks_a, relpos_b, w_out_ab, w_in_c, conv_c_w, conv_c_b,
           gate_c_wa, gate_c_ba, gate_c_wx, gate_c_bx, lambda_c, w_out_c):
    f = lambda a: np.ascontiguousarray(np.asarray(a, dtype=np.float32))
    x_prompt, x_sample = f(x_prompt), f(x_sample)
    if "nc" not in _CACHE:
        _CACHE["nc"] = build_program()
    nc = _CACHE["nc"]

    w = f(w_in_ab)[0]
    qa_cols = np.concatenate([np.arange(h * 64, (h + 1) * 64) for h in (0, 4, 1, 5, 2, 6, 3, 7)])
    cols = np.concatenate([qa_cols, np.arange(1280, 1792), np.arange(1792, 2304), np.arange(2304, 2816),
                           np.arange(768, 1280), np.arange(2816, 3328), np.arange(512, 640),
                           np.arange(640, 768)])
    w_in = np.ascontiguousarray(w[:, cols])
    tab = np.pad(f(relpos_b)[0], ((0, 0), (0, TABL - 257)), mode="edge")
    vec_rows = np.concatenate([f(conv_c_w)[0], f(conv_c_b), f(gate_c_ba), f(gate_c_bx), f(lambda_c),
                               f(ln_pre)], axis=0)
    vecs = np.ascontiguousarray(vec_rows.reshape(80, 128))
    rc, rs = _rope_tables()
    sid = np.arange(128) // 32
    cmask = (sid[:, None] == sid[None, :]).astype(np.float32)
    shared = {
        "vecs": vecs, "lnpost": f(ln_post), "w_in": w_in, "w_out0": f(w_out_ab)[0], "w_inc": f(w_in_c)[0],
        "w_out1": f(w_out_c)[0], "sinks": f(sinks_a)[0], "tab": np.ascontiguousarray(tab),
        "gwa": f(gate_c_wa)[0], "gwx": f(gate_c_wx)[0], "ropec": rc, "ropes": rs, "cmask": cmask,
    }
    cak, cav = f(cache_a_k)[0].reshape(32, 128, 128), f(cache_a_v)[0].reshape(32, 128, 128)
    cbk, cbv = f(cache_b_k)[0].reshape(32, 512, 512), f(cache_b_v)[0].reshape(32, 512, 512)
    st4 = np.concatenate([f(state_c_h)[0][:, None, :], f(state_c_conv)[0]], axis=1)
    in_maps = []
    for c in range(NCORES):
        d = dict(shared)
        d["xp"] = np.ascontiguousarray(x_prompt[2 * c:2 * c + 2])
        d["xs"] = np.ascontiguousarray(x_sample[4 * c:4 * c + 4].reshape(128, D))
        d["cak"] = np.ascontiguousarray(cak[4 * c:4 * c + 4])
        d["cav"] = np.ascontiguousarray(cav[4 * c:4 * c + 4])
        d["cbk"] = np.ascontiguousarray(cbk[4 * c:4 * c + 4])
        d["cbv"] = np.ascontiguousarray(cbv[4 * c:4 * c + 4])
        d["st4"] = np.ascontiguousarray(st4[4 * c:4 * c + 4])
        in_maps.append(d)
    res = run_bass_kernel_spmd(nc, in_maps, core_ids=list(range(NCORES)))
    R = res.results
    cat = lambda k: np.concatenate([r[k] for r in R], axis=0)
    y_p = cat("o_yp")
    y_s = cat("o_ys").reshape(32, 32, D)
    akp = cat("o_akp").reshape(1, 16, 128, 2, 64)
    avp = cat("o_avp").reshape(1, 16, 128, 2, 64)
    bkp = cat("o_bkp").reshape(1, 16, 512, 8, 64)
    bvp = cat("o_bvp").reshape(1, 16, 512, 8, 64)
    chp = cat("o_chp").reshape(1, 16, D)
    ccp = cat("o_ccp").reshape(1, 16, 3, D)
    aks = cat("o_aks").reshape(1, 32, 128, 2, 64)
    avs = cat("o_avs").reshape(1, 32, 128, 2, 64)
    bks = cat("o_bks").reshape(1, 32, 512, 8, 64)
    bvs = cat("o_bvs").reshape(1, 32, 512, 8, 64)
    chs = cat("o_chs").reshape(1, 32, D)
    ccs = cat("o_ccs").reshape(1, 32, 3, D)
    return (y_p, y_s, akp, avp, bkp, bvp, chp, ccp, aks, avs, bks, bvs, chs, ccs)
```
